# Optimizing a Trainium2 kernel written in Bass

```python
import jax, jax.numpy as jnp
from jax import lax
import numpy as np

D_MODEL = 1024
BATCH = 16
SEQ = 2048
DEPTH = 1

CHUNK = 64
PLE_DIM = 256
CONV_WIDTH = D_MODEL // 2
CONV_GROUPS = 8
CONV_K = 3
RWKV_WIDTH = D_MODEL - CONV_WIDTH
RWKV_HEAD = 64
RWKV_HEADS = RWKV_WIDTH // RWKV_HEAD
DECAY_LORA = 64
AICL_LORA = 64
GATE_LORA = 128
D_FF = 2816
FFN_CONV_K = 3
NORM_EPS = 1e-6
GN_EPS = 64e-5
RWKV_COLS = 3 * RWKV_WIDTH + DECAY_LORA + AICL_LORA + GATE_LORA
IN_COLS = 3 * CONV_WIDTH + RWKV_COLS

kernel_name = "hymba_shortconv_rwkv7_convffn_ple"


def rms_norm(x, g):
    xf = x.astype(jnp.float32)
    y = xf * lax.rsqrt(jnp.mean(xf * xf, axis=-1, keepdims=True) + NORM_EPS)
    return (y * g.astype(jnp.float32)).astype(x.dtype)


def causal_dwconv(x, w):
    k_width = w.shape[0]
    s = x.shape[1]
    xp = jnp.pad(x, ((0, 0), (k_width - 1, 0), (0, 0)))
    y = xp[:, k_width - 1:k_width - 1 + s] * w[k_width - 1]
    for j in range(k_width - 1):
        y = y + xp[:, j:j + s] * w[j]
    return y


def token_shift(z):
    return jnp.pad(z, ((0, 0), (1, 0), (0, 0)))[:, :-1]


def short_conv_mixer(z, conv_w):
    x_in, b_gate, c_gate = jnp.split(z, 3, axis=-1)
    return b_gate * causal_dwconv(c_gate * x_in, conv_w)


def wkv7_scan(r, w, k, v, kk, a):
    b, s, h, n = r.shape
    n_chunks = s // CHUNK

    def to_chunks(t):
        return t.reshape(b, n_chunks, CHUNK, h, n).transpose(1, 2, 0, 3, 4)

    xs = (to_chunks(r), to_chunks(w), to_chunks(k), to_chunks(v), to_chunks(kk), to_chunks(a))

    def step(state, inp):
        r_t, w_t, k_t, v_t, kk_t, a_t = inp
        sa = jnp.einsum('bhvk,bhk->bhv', state, -kk_t)
        state = (state * w_t[:, :, None, :]
                 + sa[..., None] * (kk_t * a_t)[:, :, None, :]
                 + v_t[..., None] * k_t[:, :, None, :])
        out = jnp.einsum('bhvk,bhk->bhv', state, r_t)
        return state, out

    def chunk_step(state, chunk_inp):
        return lax.scan(step, state, chunk_inp)

    state0 = jnp.zeros((b, h, n, n), jnp.float32)
    _, out = lax.scan(chunk_step, state0, xs)
    return out.transpose(2, 0, 1, 3, 4).reshape(b, s, h, n)


def rwkv7_mixer(z, mu, w0, w_up, a0, a_up, g_up, k_k, k_a, r_k, gn_w, gn_b):
    dtype = z.dtype
    bsz, s, _ = z.shape
    z = z.astype(jnp.float32)
    z = z + (token_shift(z) - z) * mu.astype(jnp.float32)
    c1 = RWKV_WIDTH
    r, k, v, wd, ad, gd = jnp.split(
        z, [c1, 2 * c1, 3 * c1, 3 * c1 + DECAY_LORA, 3 * c1 + DECAY_LORA + AICL_LORA], axis=-1)
    f32 = lambda t: t.astype(jnp.float32)
    w_log = -jax.nn.softplus(-(f32(w0) + jnp.tanh(wd) @ f32(w_up))) - 0.5
    decay = jnp.exp(-jnp.exp(w_log))
    a = jax.nn.sigmoid(f32(a0) + ad @ f32(a_up))
    g = jax.nn.sigmoid(gd) @ f32(g_up)
    heads = lambda t: t.reshape(bsz, s, RWKV_HEADS, RWKV_HEAD)
    kk = heads(k * f32(k_k))
    kk = kk * lax.rsqrt(jnp.maximum(jnp.sum(kk * kk, axis=-1, keepdims=True), 1e-24))
    k = k * (1.0 + (a - 1.0) * f32(k_a))
    rh, kh, vh, ah, wh = heads(r), heads(k), heads(v), heads(a), heads(decay)
    o = wkv7_scan(rh, wh, kh, vh, kk, ah)
    mean = jnp.mean(o, axis=-1, keepdims=True)
    var = jnp.mean(jnp.square(o - mean), axis=-1, keepdims=True)
    o = (o - mean) * lax.rsqrt(var + GN_EPS)
    o = o * heads(jnp.broadcast_to(f32(gn_w), (bsz, s, RWKV_WIDTH))) + heads(
        jnp.broadcast_to(f32(gn_b), (bsz, s, RWKV_WIDTH)))
    bonus = jnp.sum(rh * kh * f32(r_k), axis=-1, keepdims=True) * vh
    o = (o + bonus).reshape(bsz, s, RWKV_WIDTH) * g
    return o.astype(dtype)


def conv_glu_ffn(h, w_up, conv_w, conv_b, w_down):
    u = causal_dwconv(h @ w_up, conv_w) + conv_b
    gate, val = jnp.split(u, 2, axis=-1)
    return (jax.nn.silu(gate) * val) @ w_down


def setup_inputs(seed: int = 0) -> dict:
    key = jax.random.key(seed)
    ks = jax.random.split(key, 32)
    n = lambda k, shape, scale: jax.random.normal(k, shape, jnp.float32) * scale
    gain = lambda k, shape: 1.0 + 0.02 * jax.random.normal(k, shape, jnp.float32)
    L = DEPTH
    return {
        "x": n(ks[0], (BATCH, SEQ, D_MODEL), 1.0),
        "p": n(ks[1], (DEPTH, BATCH, SEQ, PLE_DIM), 1.0),
        "mix_norm_g": gain(ks[2], (L, D_MODEL)),
        "w_in": n(ks[3], (L, D_MODEL, IN_COLS), D_MODEL ** -0.5),
        "conv_mix_w": n(ks[4], (L, CONV_K, CONV_WIDTH), CONV_K ** -0.5),
        "rwkv_mu": jax.random.uniform(ks[5], (L, RWKV_COLS), jnp.float32),
        "rwkv_w0": jax.random.uniform(ks[6], (L, RWKV_WIDTH), jnp.float32, -4.0, 1.0),
        "rwkv_w_up": n(ks[7], (L, DECAY_LORA, RWKV_WIDTH), 0.1 * DECAY_LORA ** -0.5),
        "rwkv_a0": n(ks[8], (L, RWKV_WIDTH), 0.5),
        "rwkv_a_up": n(ks[9], (L, AICL_LORA, RWKV_WIDTH), 0.5 * AICL_LORA ** -0.5),
        "rwkv_g_up": n(ks[10], (L, GATE_LORA, RWKV_WIDTH), GATE_LORA ** -0.5),
        "rwkv_k_k": 0.85 + n(ks[11], (L, RWKV_WIDTH), 0.05),
        "rwkv_k_a": 1.0 + n(ks[12], (L, RWKV_WIDTH), 0.05),
        "rwkv_r_k": n(ks[13], (L, RWKV_HEADS, RWKV_HEAD), 0.1),
        "rwkv_gn_w": gain(ks[14], (L, RWKV_WIDTH)),
        "rwkv_gn_b": n(ks[15], (L, RWKV_WIDTH), 0.02),
        "w_out": n(ks[16], (L, D_MODEL, D_MODEL), D_MODEL ** -0.5),
        "ffn_norm_g": gain(ks[17], (L, D_MODEL)),
        "ffn_w_up": n(ks[18], (L, D_MODEL, 2 * D_FF), D_MODEL ** -0.5),
        "ffn_conv_w": n(ks[19], (L, FFN_CONV_K, 2 * D_FF), FFN_CONV_K ** -0.5),
        "ffn_conv_b": n(ks[20], (L, 2 * D_FF), 0.02),
        "ffn_w_down": n(ks[21], (L, D_FF, D_MODEL), D_FF ** -0.5),
        "ple_w_proj": n(ks[22], (L, PLE_DIM, D_MODEL), PLE_DIM ** -0.5),
        "ple_norm_g": gain(ks[23], (L, D_MODEL)),
        "ple_gate_norm_g": gain(ks[24], (L, D_MODEL)),
        "ple_w_gate": n(ks[25], (L, D_MODEL, D_MODEL), D_MODEL ** -0.5),
        "final_norm_g": gain(ks[26], (D_MODEL,)),
    }


def reference(x, p, mix_norm_g, w_in, conv_mix_w, rwkv_mu, rwkv_w0, rwkv_w_up, rwkv_a0,
              rwkv_a_up, rwkv_g_up, rwkv_k_k, rwkv_k_a, rwkv_r_k, rwkv_gn_w, rwkv_gn_b,
              w_out, ffn_norm_g, ffn_w_up, ffn_conv_w, ffn_conv_b, ffn_w_down,
              ple_w_proj, ple_norm_g, ple_gate_norm_g, ple_w_gate, final_norm_g):
    for i in range(DEPTH):
        h = rms_norm(x, mix_norm_g[i])
        z = h @ w_in[i]
        z_conv, z_rwkv = z[..., :3 * CONV_WIDTH], z[..., 3 * CONV_WIDTH:]
        y_conv = short_conv_mixer(z_conv, conv_mix_w[i])
        y_rwkv = rwkv7_mixer(z_rwkv, rwkv_mu[i], rwkv_w0[i], rwkv_w_up[i], rwkv_a0[i],
                             rwkv_a_up[i], rwkv_g_up[i], rwkv_k_k[i], rwkv_k_a[i],
                             rwkv_r_k[i], rwkv_gn_w[i], rwkv_gn_b[i])
        x = x + jnp.concatenate([y_conv, y_rwkv], axis=-1) @ w_out[i]
        x = x + conv_glu_ffn(rms_norm(x, ffn_norm_g[i]), ffn_w_up[i], ffn_conv_w[i],
                             ffn_conv_b[i], ffn_w_down[i])
        e = rms_norm(p[i] @ ple_w_proj[i], ple_norm_g[i])
        gate = jax.nn.sigmoid(rms_norm(x, ple_gate_norm_g[i]) @ ple_w_gate[i])
        x = x + gate * e
    return rms_norm(x, final_norm_g)
```

```python
import math
import os
_DBG = os.environ.get('DBGSKIP', '')
from collections import deque
from contextlib import ExitStack

import numpy as np
import concourse.bass as bass
import concourse.mybir as mybir
from concourse.bass_utils import run_bass_kernel_spmd

F32 = mybir.dt.float32
F32R = mybir.dt.float32r
BF16 = mybir.dt.bfloat16
AF = mybir.ActivationFunctionType
ALU = mybir.AluOpType

D = 1024
TB = 512
NBLK = 8
TOK = NBLK * TB
DFF = 2816
NFF = 22
SLABW = 4096
NORM_EPS = 1e-6
GN_EPS = 64e-5


def _vec_cols(a):
    a = np.asarray(a, np.float32).reshape(-1)
    return a.reshape(-1, 128).T


def _slab(W, cols):
    K = W.shape[0]
    nk = K // 128
    sub = W[:, cols].reshape(nk, 128, len(cols)).transpose(1, 0, 2).reshape(128, nk * len(cols))
    out = np.zeros((128, SLABW), np.float32)
    out[:, : sub.shape[1]] = sub
    return out


def _r(a, b):
    return list(range(a, b))


def _slab_plan():
    plan = []
    for j in range(4):
        plan.append(("conv", 8, 384))
    plan.append(("lora", 8, 256))
    for hp in range(4):
        plan.append(("rwkv", 8, 384))
    for ch in range(2):
        plan.append(("wout", 8, 512))
    for hf in range(2):
        for s in range(5):
            plan.append(("ffup", 8, 512))
        plan.append(("ffup", 8, 256))
        for ch in range(2):
            plan.append(("ffdn", 6, 512))
            plan.append(("ffdn", 5, 512))
    for ch in range(2):
        plan.append(("plep", 2, 512))
    for ch in range(2):
        plan.append(("pleg", 8, 512))
    return plan


def _pack_weights(w_in, w_out, ffn_w_up, ffn_w_down, ple_w_proj, ple_w_gate):
    slabs = []
    for j in range(4):
        slabs.append(_slab(w_in, _r(j * 128, j * 128 + 128) + _r(512 + j * 128, 512 + j * 128 + 128)
                           + _r(1024 + j * 128, 1024 + j * 128 + 128)))
    slabs.append(_slab(w_in, _r(1536 + 1536, 1536 + 1792)))
    for hp in range(4):
        b = 1536
        slabs.append(_slab(w_in, _r(b + hp * 128, b + hp * 128 + 128) + _r(b + 512 + hp * 128, b + 512 + hp * 128 + 128)
                           + _r(b + 1024 + hp * 128, b + 1024 + hp * 128 + 128)))
    for ch in range(2):
        slabs.append(_slab(w_out, _r(ch * 512, ch * 512 + 512)))
    for hf in range(2):
        for s in range(6):
            js = [11 * hf + 2 * s, 11 * hf + 2 * s + 1] if s < 5 else [11 * hf + 10]
            cols = []
            for j in js:
                cols += _r(j * 128, j * 128 + 128) + _r(DFF + j * 128, DFF + j * 128 + 128)
            slabs.append(_slab(ffn_w_up, cols))
        for ch in range(2):
            for (ka, kb) in ((0, 6), (6, 11)):
                k0 = (11 * hf + ka) * 128
                k1 = (11 * hf + kb) * 128
                slabs.append(_slab(ffn_w_down[k0:k1], _r(ch * 512, ch * 512 + 512)))
    for ch in range(2):
        slabs.append(_slab(ple_w_proj, _r(ch * 512, ch * 512 + 512)))
    for ch in range(2):
        slabs.append(_slab(ple_w_gate, _r(ch * 512, ch * 512 + 512)))
    return np.stack(slabs, 0)


def _const_layout():
    items = [("ident", 128), ("ones", 128), ("BO", 128), ("mask1", 512), ("mask3", 128),
             ("reset", 512), ("I64x", 64), ("LW", 512), ("GU", 512),
             ("mix_g", 8), ("conv_w", 12), ("mu", 14), ("w0", 4), ("a0", 4), ("k_k", 4), ("k_a", 4),
             ("r_k", 4), ("gn_w", 4), ("gn_b", 4), ("ffn_g", 8), ("fcw", 132), ("fcb", 44),
             ("ple_g", 8), ("pleg_g", 8), ("fin_g", 8)]
    off = {}
    o = 0
    for n, w in items:
        off[n] = (o, w)
        o += w
    return off, o


def _pack_consts(inp):
    off, ncol = _const_layout()
    c = np.zeros((128, ncol), np.float32)

    def put(name, arr):
        o, w = off[name]
        assert arr.shape == (128, w), (name, arr.shape, w)
        c[:, o:o + w] = arr

    i = np.arange(128)
    put("ident", np.eye(128, dtype=np.float32))
    put("ones", np.ones((128, 128), np.float32))
    put("BO", (i[:, None] // 64 == i[None, :] // 64).astype(np.float32))
    same = (i[:, None] // 64 == i[None, :] // 64)
    su = (same & (i[:, None] < i[None, :])).astype(np.float32)
    sl = (same & (i[:, None] > i[None, :])).astype(np.float32)
    u = (same & (i[:, None] <= i[None, :])).astype(np.float32)
    put("mask1", np.concatenate([su, sl, su, u], 1))
    put("mask3", u)
    rs = np.ones((128, 512), np.float32)
    rs[:, 0::64] = 0.0
    put("reset", rs)
    put("I64x", (i[:, None] % 64 == np.arange(64)[None, :]).astype(np.float32))
    lw = np.zeros((128, 512), np.float32)
    lw[0:64] = inp["rwkv_w_up"][0]
    lw[64:128] = inp["rwkv_a_up"][0]
    put("LW", lw)
    put("GU", np.asarray(inp["rwkv_g_up"][0], np.float32))
    put("mix_g", _vec_cols(inp["mix_norm_g"][0]))
    put("conv_w", np.concatenate([_vec_cols(inp["conv_mix_w"][0][t]) for t in range(3)], 1))
    put("mu", _vec_cols(inp["rwkv_mu"][0]))
    put("w0", _vec_cols(inp["rwkv_w0"][0]))
    put("a0", _vec_cols(inp["rwkv_a0"][0]))
    put("k_k", _vec_cols(inp["rwkv_k_k"][0]))
    put("k_a", _vec_cols(inp["rwkv_k_a"][0]))
    put("r_k", _vec_cols(inp["rwkv_r_k"][0]))
    put("gn_w", _vec_cols(inp["rwkv_gn_w"][0]))
    put("gn_b", _vec_cols(inp["rwkv_gn_b"][0]))
    put("ffn_g", _vec_cols(inp["ffn_norm_g"][0]))
    put("fcw", np.concatenate([_vec_cols(inp["ffn_conv_w"][0][t]) for t in range(3)], 1))
    put("fcb", _vec_cols(inp["ffn_conv_b"][0]))
    put("ple_g", _vec_cols(inp["ple_norm_g"][0]))
    put("pleg_g", _vec_cols(inp["ple_gate_norm_g"][0]))
    put("fin_g", _vec_cols(inp["final_norm_g"]))
    return c


class TL:
    def __init__(self, h, name):
        self.h = h
        self.name = name
        self.last_w = None
        self.readers = []

    def __getitem__(self, k):
        return V(self, self.h[k])


class V:
    def __init__(self, tl, ap):
        self.tl = tl
        self.ap = ap

    def __getitem__(self, k):
        return V(self.tl, self.ap[k])

    def bc(self, dt):
        return V(self.tl, self.ap.bitcast(dt))


class Op:
    __slots__ = ("eng", "fn", "deps", "signal", "sem", "tick", "is_dma", "waits", "idx")


class Sched:
    ENG = ("pe", "act", "dve", "pool", "sp")

    def __init__(self):
        self.streams = {e: [] for e in self.ENG}
        self.ops = []
        self.dma_groups = []

    def emit(self, eng, fn, reads=(), writes=(), dma_sem=None, extra_deps=()):
        op = Op()
        op.eng = eng
        op.fn = fn
        op.signal = dma_sem is not None
        op.sem = dma_sem
        op.tick = None
        op.is_dma = dma_sem is not None
        op.waits = []
        deps = set(extra_deps)
        rt = []
        for v in reads:
            t = v.tl if isinstance(v, V) else (v if isinstance(v, TL) else None)
            if t is not None and t not in rt:
                rt.append(t)
        wt = []
        for v in writes:
            t = v.tl if isinstance(v, V) else v
            if t not in wt:
                wt.append(t)
        for t in rt:
            if t.last_w is not None:
                deps.add(t.last_w)
        for t in wt:
            if t.last_w is not None:
                deps.add(t.last_w)
            for r in t.readers:
                deps.add(r)
        for t in rt:
            t.readers.append(op)
        for t in wt:
            t.last_w = op
            t.readers = []
        deps.discard(op)
        op.deps = deps
        op.idx = len(self.ops)
        self.ops.append(op)
        self.streams[eng].append(op)
        return op

    def finalize(self, sems):
        for op in self.ops:
            for d in list(op.deps):
                if d.is_dma:
                    continue
                if d.eng == "pe" and op.eng == "pe" and not op.is_dma:
                    op.deps.discard(d)
                    continue
                d.signal = True
        cnt = {}
        for op in self.ops:
            if op.is_dma:
                cnt[op.sem] = cnt.get(op.sem, 0) + 16
                op.tick = cnt[op.sem]
        for grp in self.dma_groups:
            m = max(o.tick for o in grp)
            gs = set(grp)
            for o in grp:
                o.tick = m
                o.deps -= gs
        for e in self.ENG:
            c = 0
            for op in self.streams[e]:
                if op.is_dma:
                    continue
                if op.signal:
                    c += 1
                    op.tick = c
                    op.sem = e
        for e in self.ENG:
            waited = {}
            for op in self.streams[e]:
                need = {}
                for d in op.deps:
                    need[d.sem] = max(need.get(d.sem, 0), d.tick)
                for s, v in need.items():
                    if waited.get(s, 0) >= v:
                        continue
                    waited[s] = v
                    op.waits.append((s, v))

    def replay(self, eng, e, sems):
        for op in self.streams[eng]:
            for s, v in op.waits:
                e.wait_ge(sems[s], v)
            if op.fn is None:
                continue
            ins = op.fn(e)
            if op.signal:
                ins.then_inc(sems[op.sem], 16 if op.is_dma else 1)


class Pool_:
    def __init__(self, tiles):
        self.q = deque(tiles)

    def get(self):
        return self.q.popleft()

    def put(self, *ts):
        for t in ts:
            self.q.append(t)


class _Stop(Exception):
    pass


def build_nc(nblk=NBLK, taps=(), stop=None):
    nc = bass.Bass("TRN2", target_bir_lowering=False)
    nc.dge_precook = False
    plan = _slab_plan()
    nslab = len(plan)
    coff, ncol = _const_layout()
    xT_d = nc.dram_tensor("xT", [D, TOK], F32, kind="ExternalInput").ap()
    pT_d = nc.dram_tensor("pT", [256, TOK], F32R, kind="ExternalInput").ap()
    wp_d = nc.dram_tensor("wpack", [nslab, 128, SLABW], F32R, kind="ExternalInput").ap()
    cp_d = nc.dram_tensor("cpack", [128, ncol], F32, kind="ExternalInput").ap()
    out_d = nc.dram_tensor("outT", [D, TOK], F32, kind="ExternalOutput").ap()
    tap_d = {}
    for name, shape in taps:
        tap_d[name] = nc.dram_tensor("tap_" + name, list(shape), F32, kind="ExternalOutput").ap()

    S = Sched()
    NFP = 21
    NRP = 12
    NBP = 24
    with ExitStack() as st:
        def sb(name, shape, dt):
            return st.enter_context(nc.sbuf_tensor(name, shape, dt))

        cst_h = sb("cst", [128, ncol], F32)
        cst = TL(cst_h, "cst")
        xall = sb("xall", [128, 8, TB], F32)
        X = [TL(xall[:, c, :], f"x{c}") for c in range(8)]
        hall = sb("hall", [128, 8, TB], F32)
        H = [TL(hall[:, c, :], f"h{c}") for c in range(8)]
        pall = sb("pall", [128, 2, TB], F32R)
        PT = [TL(pall[:, c, :], f"p{c}") for c in range(2)]
        ring_h = sb("ring", [128, 3, SLABW], F32R)
        RING = [TL(ring_h[:, i, :], f"ring{i}") for i in range(3)]
        fp_h = sb("fpool", [128, NFP, 516], F32)
        FP = Pool_([TL(fp_h[:, i, :], f"fp{i}") for i in range(NFP)])
        rp_h = sb("rpool", [128, NRP, TB], F32R)
        RP = Pool_([TL(rp_h[:, i, :], f"rp{i}") for i in range(NRP)])
        bp_h = sb("bpool", [128, NBP, 512], BF16)
        BP = Pool_([TL(bp_h[:, i, :], f"bp{i}") for i in range(NBP)])
        tm_h = sb("tm", [128, 2048], BF16)
        TM = TL(tm_h, "tm")
        cb_h = sb("cbf", [128, 384], BF16)
        CB = TL(cb_h, "cbf")
        misc_h = sb("misc", [128, 16], F32)
        MISC = TL(misc_h, "misc")
        hc_h = sb("hconv", [128, 4, 2], F32)
        HCONV = TL(hc_h, "hconv")
        hr_h = sb("hrw", [128, 14], F32)
        HRW = TL(hr_h, "hrw")
        hf_h = sb("hffn", [128, 44, 2], F32)
        HFFN = TL(hf_h, "hffn")
        stw_h = sb("stw", [128, 9 * 128], F32)
        STW = TL(stw_h, "stw")
        car_h = sb("carry", [128, 4, 128], F32)
        CARRY = [TL(car_h[:, hp, :], f"carry{hp}") for hp in range(4)]
        ps_h = [st.enter_context(nc.psum_tensor(f"ps{i}", [128, 512], F32)) for i in range(6)]
        PS = Pool_([TL(ps_h[i], f"ps{i}") for i in range(6)])
        psb_h = [st.enter_context(nc.psum_tensor(f"psb{i}", [128, 1024], BF16)) for i in range(2)]
        PSB = Pool_([TL(psb_h[i], f"psb{i}") for i in range(2)])

        sem_names = ["pe", "act", "dve", "pool", "sp", "ring0", "ring1", "ring2", "x", "p", "out0", "out1", "const"]
        sems = {n: st.enter_context(nc.semaphore(n)) for n in sem_names}

        def cc(name, j=0, w=1):
            o, _ = coff[name]
            return cst[:, o + j:o + j + w]

        def mm(out, lhsT, rhs, start=True, stop=True):
            S.emit("pe", lambda e: e.matmul(out.ap, lhsT.ap, rhs.ap, start=start, stop=stop),
                   [lhsT, rhs], [out])

        def tr(out, in_, ident):
            S.emit("pe", lambda e: e.transpose(out.ap, in_.ap, ident.ap), [in_, ident], [out])

        def act(out, in_, func, bias=None, scale=None):
            kw = {}
            rd = [in_]
            if bias is not None:
                kw["bias"] = bias.ap if isinstance(bias, V) else bias
                rd.append(bias)
            if scale is not None:
                kw["scale"] = scale.ap if isinstance(scale, V) else scale
                rd.append(scale)
            S.emit("act", lambda e: e.activation(out.ap, in_.ap, func, **kw), rd, [out])

        def tt(eng, out, a, b, op):
            S.emit(eng, lambda e: e.tensor_tensor(out.ap, a.ap, b.ap, op), [a, b], [out])

        def ts(eng, out, a, s1, s2, op0, op1=None):
            rd = [a, s1, s2]
            s1a = s1.ap if isinstance(s1, V) else s1
            s2a = s2.ap if isinstance(s2, V) else s2
            if op1 is None:
                S.emit(eng, lambda e: e.tensor_scalar(out.ap, a.ap, s1a, None, op0), rd, [out])
            else:
                S.emit(eng, lambda e: e.tensor_scalar(out.ap, a.ap, s1a, s2a, op0, op1), rd, [out])

        def stt(out, a, s, b, op0, op1):
            sa = s.ap if isinstance(s, V) else s
            S.emit("dve", lambda e: e.scalar_tensor_tensor(out.ap, a.ap, sa, b.ap, op0, op1), [a, s, b], [out])

        def cp(eng, out, a):
            if eng == "act":
                S.emit("act", lambda e: e.activation(out.ap, a.ap, AF.Copy), [a], [out])
            else:
                S.emit(eng, lambda e: e.tensor_copy(out.ap, a.ap), [a], [out])

        def recip(out, a):
            S.emit("dve", lambda e: e.reciprocal(out.ap, a.ap), [a], [out])

        def memset(eng, out, val):
            S.emit(eng, lambda e: e.memset(out.ap, val), [], [out])

        def dma(eng, out, in_, sem, reads=(), writes=(), extra=()):
            oa = out.ap if isinstance(out, V) else out
            ia = in_.ap if isinstance(in_, V) else in_
            return S.emit(eng, lambda e: e.dma_start(out=oa, in_=ia), reads, writes, dma_sem=sem, extra_deps=extra)

        def tap(name, v):
            if name in tap_d:
                dma("sp", tap_d[name], v, "const", reads=[v])

        N = slice(0, TB)

        def ck(k):
            if stop is not None and stop == k:
                raise _Stop()

        g = []
        half = ncol // 2
        g.append(dma("sp", cst[:, 0:half], cp_d[:, 0:half], "const", writes=[cst[:, :]]))
        g.append(dma("sp", cst[:, half:ncol], cp_d[:, half:ncol], "const", writes=[cst[:, :]]))
        S.dma_groups.append(g)
        cp("dve", CB[:, 0:128], cc("ident", 0, 128))
        cp("dve", CB[:, 128:256], cc("ident", 0, 128))
        cp("dve", CB[:, 256:384], cc("ident", 0, 128))
        IDB = CB[:, 0:128]
        I2 = CB[:, 128:384]
        ts("dve", MISC[:, 0:4], cc("k_a", 0, 4), -1.0, 1.0, ALU.mult, ALU.add)
        ones_h = sb("onesr", [128, 128], F32R)
        ONESR = TL(ones_h, "onesr")
        if 'o' not in _DBG:
            cp("dve", ONESR[:, :], cc("ones", 0, 128))
        ONES_R = ONESR[:, :]

        slab_ctr = [0]

        def load_slab(blk_slab_idx):
            kind, nk, ncols = plan[blk_slab_idx]
            i = slab_ctr[0]
            slab_ctr[0] += 1
            slot = RING[i % 3]
            w = nk * ncols
            h2 = w // 2
            grp = [dma("sp", slot[:, 0:h2], wp_d[blk_slab_idx, :, 0:h2], f"ring{i % 3}", writes=[slot]),
                   dma("sp", slot[:, h2:w], wp_d[blk_slab_idx, :, h2:w], f"ring{i % 3}", writes=[slot])]
            S.dma_groups.append(grp)
            return slot, nk, ncols

        def wv(slot, ncols, k, m):
            return slot[:, k * ncols + m * 128:k * ncols + m * 128 + 128]

        bslab = [0]

        def next_slab():
            r = load_slab(bslab[0])
            bslab[0] += 1
            return r

        def project(rhs, consume):
            slot, nk, ncols = next_slab()
            for m in range(ncols // 128):
                ps = PS.get()
                for k in range(nk):
                    mm(ps[:, N], wv(slot, ncols, k, m), (rhs[k][:, N].bc(F32R) if rhs is H else rhs[k][:, N]), start=(k == 0), stop=(k == nk - 1))
                consume(m, ps)
                PS.put(ps)

        def rms_norm(xs, gname, outs):
            ps = PS.get()
            for c in range(8):
                sq = RP.get()
                act(sq[:, N], xs[c][:, N], AF.Square)
                mm(ps[:, N], ONES_R, sq[:, N], start=(c == 0), stop=(c == 7))
                RP.put(sq)
            r = FP.get()
            ts("dve", r[:, N], ps[:, N], 1.0 / D, NORM_EPS, ALU.mult, ALU.add)
            PS.put(ps)
            act(r[:, N], r[:, N], AF.Sqrt)
            recip(r[:, N], r[:, N])
            for c in range(8):
                stt(outs[c][:, N].bc(F32R), xs[c][:, N], cc(gname, c), r[:, N], ALU.mult, ALU.mult)
            FP.put(r)

        out_grp_prev = {0: None, 1: None}

        for bi in range(nblk):
            seq_start = (bi % 4 == 0)
            t0 = bi * TB
            bslab[0] = 0
            S.dma_groups.append([dma("sp" if 'x' in _DBG else "act", X[c][:, N], xT_d[c * 128:c * 128 + 128, t0:t0 + TB], "x", writes=[X[c]])
                                 for c in range(8)])
            if 'p' not in _DBG:
                S.dma_groups.append([dma("act", PT[c][:, N], pT_d[c * 128:c * 128 + 128, t0:t0 + TB], "p", writes=[PT[c]])
                                     for c in range(2)])
            if seq_start and 'm' not in _DBG:
                memset("dve", HCONV[:, :, :], 0.0)
                memset("dve", HRW[:, :], 0.0)
                memset("dve", HFFN[:, :, :], 0.0)
                for hp in range(4):
                    memset("dve", CARRY[hp][:, :], 0.0)

            try:
                ck(0)
                rms_norm(X, "mix_g", H)
                if bi == 0:
                    tap("h0", H[0][:, N])
                ck(1)
                Y = [RP.get() for _ in range(8)]

                for j in range(4):
                    hold = {}

                    def consume(m, ps, j=j, hold=hold):
                        if m == 0:
                            hold["xin"] = FP.get()
                            cp("act", hold["xin"][:, N], ps[:, N])
                        elif m == 1:
                            hold["b"] = FP.get()
                            cp("act", hold["b"][:, N], ps[:, N])
                        else:
                            cx = FP.get()
                            tt("dve", cx[:, 2:2 + TB], ps[:, N], hold["xin"][:, N], ALU.mult)
                            cp("dve", cx[:, 0:2], HCONV[:, j, :])
                            cp("dve", HCONV[:, j, :], cx[:, TB:TB + 2])
                            acc = FP.get()
                            act(acc[:, N], cx[:, 2:2 + TB], AF.Identity, scale=cc("conv_w", 8 + j))
                            stt(acc[:, N], cx[:, 1:1 + TB], cc("conv_w", 4 + j), acc[:, N], ALU.mult, ALU.add)
                            stt(acc[:, N], cx[:, 0:TB], cc("conv_w", j), acc[:, N], ALU.mult, ALU.add)
                            tt("dve", Y[j][:, N], hold["b"][:, N], acc[:, N], ALU.mult)
                            FP.put(cx, acc, hold["xin"], hold["b"])

                    project(H, consume)
                if bi == 0:
                    tap("yc0", Y[0][:, N])
                ck(2)

                def lerp_evac(ps, cid):
                    zs = FP.get()
                    cp("act", zs[:, 1:1 + TB], ps[:, N])
                    cp("dve", zs[:, 0:1], HRW[:, cid:cid + 1])
                    cp("dve", HRW[:, cid:cid + 1], zs[:, TB:TB + 1])
                    d = FP.get()
                    tt("pool", d[:, N], zs[:, 0:TB], zs[:, 1:1 + TB], ALU.subtract)
                    zl = FP.get()
                    stt(zl[:, N], d[:, N], cc("mu", cid), zs[:, 1:1 + TB], ALU.mult, ALU.add)
                    FP.put(zs, d)
                    return zl

                lora = {}

                def consume_lora(m, ps):
                    lora[m] = lerp_evac(ps, 12 + m)

                project(H, consume_lora)
                z12, z13 = lora[0], lora[1]
                tw = FP.get()
                act(tw[0:64, N], z12[0:64, N], AF.Tanh)
                sg = FP.get()
                act(sg[:, N], z13[:, N], AF.Sigmoid)
                FP.put(z13)
                ck(3)

                for hp in range(4):
                    zz = {}

                    def consume_r(m, ps, hp=hp, zz=zz):
                        zz[m] = lerp_evac(ps, m * 4 + hp)

                    project(H, consume_r)
                    zr, zk, zv = zz[0], zz[1], zz[2]
                    LWo = coff["LW"][0]
                    GUo = coff["GU"][0]
                    ps = PS.get()
                    mm(ps[:, N], cst[0:64, LWo + hp * 128:LWo + hp * 128 + 128], tw[0:64, N])
                    lw = FP.get()
                    act(lw[:, N], ps[:, N], AF.Sigmoid, bias=cc("w0", hp))
                    PS.put(ps)
                    ts("dve", lw[:, N], lw[:, N], -math.exp(-0.5), None, ALU.mult)
                    logP = FP.get()
                    S.emit("dve", lambda e, o=logP[:, N], d0=cc("reset", 0, 512), d1=lw[:, N]:
                           e.tensor_tensor_scan(o.ap, d0.ap, d1.ap, 0.0, ALU.mult, ALU.add),
                           [cc("reset", 0, 512), lw[:, N]], [logP[:, N]])
                    ps = PS.get()
                    mm(ps[:, N], cst[64:128, LWo + hp * 128:LWo + hp * 128 + 128], z12[64:128, N])
                    a_ = FP.get()
                    act(a_[:, N], ps[:, N], AF.Sigmoid, bias=cc("a0", hp))
                    PS.put(ps)
                    ps = PS.get()
                    mm(ps[:, N], cst[:, GUo + hp * 128:GUo + hp * 128 + 128], sg[:, N])
                    g_ = FP.get()
                    cp("act", g_[:, N], ps[:, N])
                    PS.put(ps)
                    kq = FP.get()
                    act(kq[:, N], zk[:, N], AF.Identity, scale=cc("k_k", hp))
                    sq = FP.get()
                    act(sq[:, N], kq[:, N], AF.Square)
                    ps = PS.get()
                    mm(ps[:, N], cc("BO", 0, 128), sq[:, N])
                    ts("dve", sq[:, N], ps[:, N], 1e-24, None, ALU.max)
                    PS.put(ps)
                    act(sq[:, N], sq[:, N], AF.Sqrt)
                    recip(sq[:, N], sq[:, N])
                    kk = FP.get()
                    tt("dve", kk[:, N], kq[:, N], sq[:, N], ALU.mult)
                    FP.put(kq, sq)
                    f_ = FP.get()
                    ts("dve", f_[:, N], a_[:, N], cc("k_a", hp), MISC[:, hp:hp + 1], ALU.mult, ALU.add)
                    kp = FP.get()
                    tt("pool", kp[:, N], zk[:, N], f_[:, N], ALU.mult)
                    FP.put(f_, zk)
                    ba = FP.get()
                    tt("pool", ba[:, N], kk[:, N], a_[:, N], ALU.mult)
                    FP.put(a_)
                    e1 = FP.get()
                    act(e1[:, N], logP[:, N], AF.Exp)
                    e2 = FP.get()
                    act(e2[:, N], logP[:, N], AF.Exp, scale=-1.0)
                    e3 = FP.get()
                    tt("pool", e3[:, N], logP[:, N], lw[:, N], ALU.subtract)
                    act(e3[:, N], e3[:, N], AF.Exp)
                    e4 = FP.get()
                    for c in range(8):
                        cs = slice(c * 64, c * 64 + 64)
                        act(e4[:, cs], logP[:, cs], AF.Exp, bias=logP[:, c * 64 + 63:c * 64 + 64], scale=-1.0)
                    FP.put(lw, logP)
                    Rt = BP.get()
                    tt("dve", Rt[:, N], zr[:, N], e1[:, N], ALU.mult)
                    Bt = BP.get()
                    tt("dve", Bt[:, N], ba[:, N], e2[:, N], ALU.mult)
                    Kt = BP.get()
                    tt("dve", Kt[:, N], kp[:, N], e2[:, N], ALU.mult)
                    At = BP.get()
                    stt(At[:, N], kk[:, N], -1.0, e3[:, N], ALU.mult, ALU.mult)
                    Bh = BP.get()
                    tt("dve", Bh[:, N], ba[:, N], e4[:, N], ALU.mult)
                    Kh = BP.get()
                    tt("dve", Kh[:, N], kp[:, N], e4[:, N], ALU.mult)
                    vb = BP.get()
                    cp("act", vb[:, N], zv[:, N])
                    FP.put(e2, e3, e4, ba, kk)
                    rk = FP.get()
                    stt(rk[:, N], zr[:, N], cc("r_k", hp), kp[:, N], ALU.mult, ALU.mult)
                    ps = PS.get()
                    mm(ps[:, N], cc("BO", 0, 128), rk[:, N])
                    bonus = FP.get()
                    tt("dve", bonus[:, N], ps[:, N], zv[:, N], ALU.mult)
                    PS.put(ps)
                    FP.put(rk, kp, zr, zv)
                    if bi == 0 and hp == 0:
                        tap("g0", g_[:, N])
                        tap("bonus0", bonus[:, N])
                        tap("e1", e1[:, N])
                    ck(4)

                    srcs = [At, Bh, Kh, vb]
                    for pr in range(2):
                        pb = PSB.get()
                        for t2 in range(2):
                            tti = pr * 2 + t2
                            for q in range(4):
                                tr(pb[:, t2 * 512 + q * 128:t2 * 512 + q * 128 + 128],
                                   srcs[q][:, tti * 128:tti * 128 + 128], IDB)
                        cp("act", TM[:, pr * 1024:pr * 1024 + 1024], pb[:, 0:1024])
                        PSB.put(pb)
                    BP.put(Bh, Kh, vb)
                    ck(5)

                    def tm(tti, q, h, rows=slice(0, 128)):
                        o = tti * 512 + q * 128 + h * 64
                        return TM[rows, o:o + 64]

                    def tmf(tti, q, rows=slice(0, 128)):
                        o = tti * 512 + q * 128
                        return TM[rows, o:o + 128]

                    RH = FP.get()
                    AcTs = [FP.get(), FP.get()]
                    Dts = [FP.get(), FP.get()]
                    for d_ in Dts:
                        memset("dve", d_[:, N], 0.0)

                    def acv(c):
                        return AcTs[c // 4][:, (c % 4) * 128:(c % 4) * 128 + 128]

                    def dv(c, rows=slice(0, 128), cols=slice(0, 128)):
                        o = (c % 4) * 128
                        return Dts[c // 4][rows, o + cols.start:o + cols.stop]

                    keep = []
                    for tti in range(4):
                        tok = slice(tti * 128, tti * 128 + 128)
                        bH = [PS.get(), PS.get()]
                        bK = [PS.get(), PS.get()]
                        W = []
                        MK = BP.get()
                        for h in range(2):
                            rows = slice(64 * h, 64 * h + 64)
                            mm(bH[h][:, 0:128], Bt[rows, tok], At[rows, tok])
                            mm(bH[h][:, 128:256], At[rows, tok], Bt[rows, tok])
                            mm(bH[h][:, 256:384], Kt[rows, tok], At[rows, tok])
                            mm(bH[h][:, 384:512], Bt[rows, tok], Rt[rows, tok])
                            mm(bK[h][:, 0:128], Kt[rows, tok], Rt[rows, tok])
                        for h in range(2):
                            Wh = BP.get()
                            tt("dve", Wh[:, N], bH[h][:, N], cc("mask1", 0, 512), ALU.mult)
                            tt("dve", MK[:, h * 128:h * 128 + 128], bK[h][:, 0:128], cc("mask3", 0, 128), ALU.mult)
                            W.append(Wh)
                        PS.put(*bH)
                        PS.put(*bK)
                        ck(51)
                        Tt = BP.get()
                        for h in range(2):
                            tt("dve", Tt[:, h * 128:h * 128 + 128], W[h][:, 0:128], IDB, ALU.add)
                        cur = None
                        for lev in range(5):
                            bk = PS.get()
                            nxt = BP.get()
                            for h in range(2):
                                if cur is None:
                                    Np = W[h][:, 0:128]
                                    Lp = W[h][:, 128:256]
                                else:
                                    Np = cur[:, h * 128:h * 128 + 128]
                                    Lp = cur[:, 256 + h * 128:256 + h * 128 + 128]
                                if lev < 4:
                                    mm(bk[:, h * 128:h * 128 + 128], Lp, Np)
                                mm(bk[:, 256 + h * 128:256 + h * 128 + 128], Np, Lp)
                            if lev < 4:
                                cp("act", nxt[:, N], bk[:, N])
                            else:
                                cp("act", nxt[:, 256:512], bk[:, 256:512])
                            PS.put(bk)
                            bT = PS.get()
                            for h in range(2):
                                mm(bT[:, h * 128:h * 128 + 128], nxt[:, 256 + h * 128:256 + h * 128 + 128],
                                   Tt[:, h * 128:h * 128 + 128], start=True, stop=False)
                                mm(bT[:, h * 128:h * 128 + 128], IDB, Tt[:, h * 128:h * 128 + 128], start=False, stop=True)
                            Tn = BP.get()
                            cp("dve", Tn[:, 0:256], bT[:, 0:256])
                            PS.put(bT)
                            if cur is not None:
                                BP.put(cur)
                            BP.put(Tt)
                            cur, Tt = nxt, Tn
                        BP.put(cur)
                        ck(52)
                        bV = PS.get()
                        for h in range(2):
                            mm(bV[:, h * 64:h * 64 + 64], W[h][:, 256:384], tm(tti, 3, h))
                        LV = BP.get()
                        cp("act", LV[:, 0:128], bV[:, 0:128])
                        PS.put(bV)
                        bY = PS.get()
                        for h in range(2):
                            mm(bY[:, h * 64:h * 64 + 64], Tt[:, h * 128:h * 128 + 128], tm(tti, 0, h))
                            mm(bY[:, 128 + h * 64:128 + h * 64 + 64], Tt[:, h * 128:h * 128 + 128], LV[:, h * 64:h * 64 + 64])
                        Yb = BP.get()
                        cp("dve", Yb[:, 0:256], bY[:, 0:256])
                        PS.put(bY)
                        BP.put(LV, Tt)
                        ck(53)
                        bR = PS.get()
                        for h in range(2):
                            mm(bR[64 * h:64 * h + 64, 0:128], Yb[:, h * 64:h * 64 + 64], W[h][:, 384:512])
                        tt("dve", RH[:, tok], bR[:, 0:128], Rt[:, tok], ALU.add)
                        PS.put(bR)
                        ck(54)
                        bAD = [PS.get(), PS.get()]
                        for c2 in range(2):
                            tr_ = slice(64 * c2, 64 * c2 + 64)
                            mm(bAD[c2][:, 0:128], Yb[tr_, 0:128], tmf(tti, 1, tr_))
                            for h in range(2):
                                mm(bAD[c2][:, 128 + h * 64:128 + h * 64 + 64], tmf(tti, 1, tr_),
                                   Yb[tr_, 128 + h * 64:128 + h * 64 + 64], start=True, stop=False)
                                mm(bAD[c2][:, 128 + h * 64:128 + h * 64 + 64], tmf(tti, 2, tr_), tm(tti, 3, h, tr_),
                                   start=False, stop=True)
                        ck(57)
                        for c2 in range(2):
                            c = tti * 2 + c2
                            tmpA = FP.get()
                            cp("act", tmpA[:, 0:128], bAD[c2][:, 0:128])
                            tt("pool", tmpA[:, 0:128], tmpA[:, 0:128], cc("BO", 0, 128), ALU.mult)
                            stt(acv(c), cc("ident", 0, 128), e1[:, c * 64 + 63:c * 64 + 64], tmpA[:, 0:128], ALU.mult, ALU.add)
                            FP.put(tmpA)
                            for h in range(2):
                                if 'C' in _DBG:
                                    continue
                                hr = slice(64 * h, 64 * h + 64)
                                cp("act", dv(c, hr, slice(64 * h, 64 * h + 64)), bAD[c2][hr, 128 + h * 64:128 + h * 64 + 64])
                        PS.put(*bAD)
                        ck(55)
                        keep.append((Yb, W[0], W[1], MK))
                        if tti == 1:
                            ck(56)
                    BP.put(At, Bt, Kt, Rt)
                    FP.put(e1)
                    ck(6)
                    cp("dve", STW[:, 0:128], CARRY[hp][:, :])
                    for c in range(8):
                        bS = PS.get()
                        mm(bS[:, 0:128], acv(c), STW[:, c * 128:c * 128 + 128])
                        tt("dve", STW[:, (c + 1) * 128:(c + 1) * 128 + 128], bS[:, 0:128], dv(c), ALU.add)
                        PS.put(bS)
                    cp("dve", CARRY[hp][:, :], STW[:, 1024:1152])
                    bO = PS.get()
                    for tti in range(4):
                        tok = slice(tti * 128, tti * 128 + 128)
                        Yb, W0, W1, MK = keep[tti]
                        Wl = [W0, W1]
                        for h in range(2):
                            hr = slice(64 * h, 64 * h + 64)
                            S.emit("pe", lambda e, o=bO[hr, tok], l=Yb[:, 128 + h * 64:128 + h * 64 + 64], r=Wl[h][:, 384:512]:
                                   e.matmul(o.ap, l.ap, r.ap, start=True, stop=False, skip_group_check=True),
                                   [Yb, Wl[h]], [bO])
                            S.emit("pe", lambda e, o=bO[hr, tok], l=tm(tti, 3, h), r=MK[:, h * 128:h * 128 + 128]:
                                   e.matmul(o.ap, l.ap, r.ap, start=False, stop=False, skip_group_check=True),
                                   [TM, MK], [bO])
                        for c2 in range(2):
                            c = tti * 2 + c2
                            cs = slice(c * 64, c * 64 + 64)
                            S.emit("pe", lambda e, o=bO[:, cs], l=STW[:, c * 128:c * 128 + 128], r=RH[:, cs], last=(c2 == 1):
                                   e.matmul(o.ap, l.ap, r.ap, start=False, stop=last, skip_group_check=True),
                                   [STW, RH], [bO])
                        BP.put(Yb, W0, W1, MK)
                    osb = FP.get()
                    cp("act", osb[:, N], bO[:, N])
                    PS.put(bO)
                    FP.put(RH, *AcTs, *Dts)
                    ck(7)
                    if bi == 0 and hp == 0:
                        tap("o0", osb[:, N])
                    ps = PS.get()
                    mm(ps[:, N], cc("BO", 0, 128), osb[:, N])
                    oc = FP.get()
                    stt(oc[:, N], ps[:, N], -1.0 / 64, osb[:, N], ALU.mult, ALU.add)
                    PS.put(ps)
                    sq = FP.get()
                    act(sq[:, N], oc[:, N], AF.Square)
                    ps = PS.get()
                    mm(ps[:, N], cc("BO", 0, 128), sq[:, N])
                    ts("dve", sq[:, N], ps[:, N], 1.0 / 64, GN_EPS, ALU.mult, ALU.add)
                    PS.put(ps)
                    act(sq[:, N], sq[:, N], AF.Sqrt)
                    recip(sq[:, N], sq[:, N])
                    tt("dve", oc[:, N], oc[:, N], sq[:, N], ALU.mult)
                    ts("dve", oc[:, N], oc[:, N], cc("gn_w", hp), cc("gn_b", hp), ALU.mult, ALU.add)
                    tt("pool", oc[:, N], oc[:, N], bonus[:, N], ALU.add)
                    tt("dve", Y[4 + hp][:, N], oc[:, N], g_[:, N], ALU.mult)
                    FP.put(osb, oc, sq, bonus, g_)
                FP.put(tw, sg, z12)
                if bi == 0:
                    tap("yr0", Y[4][:, N])
                ck(8)

                for ch in range(2):
                    def consume_o(m, ps, ch=ch):
                        c = ch * 4 + m
                        tt("dve", X[c][:, N], ps[:, N], X[c][:, N], ALU.add)
                    project(Y, consume_o)
                RP.put(*Y)
                if bi == 0:
                    tap("x1", X[0][:, N])
                ck(9)

                rms_norm(X, "ffn_g", H)
                for hf in range(2):
                    FA = [RP.get() for _ in range(11)]
                    for s in range(6):
                        accs = {}

                        def consume_f(m, ps, s=s, accs=accs, hf=hf, FA=FA):
                            jl = 2 * s + m // 2
                            j = 11 * hf + jl
                            q = j if (m % 2 == 0) else 22 + j
                            us = FP.get()
                            cp("act", us[:, 2:2 + TB], ps[:, N])
                            cp("dve", us[:, 0:2], HFFN[:, q, :])
                            cp("dve", HFFN[:, q, :], us[:, TB:TB + 2])
                            acc = FP.get()
                            act(acc[:, N], us[:, 2:2 + TB], AF.Identity, bias=cc("fcb", q), scale=cc("fcw", 88 + q))
                            stt(acc[:, N], us[:, 1:1 + TB], cc("fcw", 44 + q), acc[:, N], ALU.mult, ALU.add)
                            stt(acc[:, N], us[:, 0:TB], cc("fcw", q), acc[:, N], ALU.mult, ALU.add)
                            FP.put(us)
                            if m % 2 == 0:
                                accs["g"] = acc
                            else:
                                gt = accs["g"]
                                sgm = FP.get()
                                act(sgm[:, N], gt[:, N], AF.Sigmoid)
                                tt("pool", sgm[:, N], sgm[:, N], gt[:, N], ALU.mult)
                                tt("dve", FA[jl][:, N], sgm[:, N], acc[:, N], ALU.mult)
                                FP.put(sgm, gt, acc)

                        project(H, consume_f)
                    for ch in range(2):
                        banks = [PS.get() for _ in range(4)]
                        for (ka, kb) in ((0, 6), (6, 11)):
                            slot, nk, ncols = next_slab()
                            for m in range(4):
                                for k in range(nk):
                                    jl = ka + k
                                    mm(banks[m][:, N], wv(slot, ncols, k, m), FA[jl][:, N],
                                       start=(jl == 0), stop=(jl == 10))
                        for m in range(4):
                            c = ch * 4 + m
                            tt("dve", X[c][:, N], banks[m][:, N], X[c][:, N], ALU.add)
                        PS.put(*banks)
                    RP.put(*FA)
                if bi == 0:
                    tap("x2", X[0][:, N])
                ck(10)

                E = []
                for ch in range(2):
                    def consume_e(m, ps, ch=ch):
                        e_ = FP.get()
                        cp("act", e_[:, N], ps[:, N])
                        E.append(e_)
                    project(PT, consume_e)
                ps = PS.get()
                for c in range(8):
                    sq = RP.get()
                    act(sq[:, N], E[c][:, N], AF.Square)
                    mm(ps[:, N], ONES_R, sq[:, N], start=(c == 0), stop=(c == 7))
                    RP.put(sq)
                re_ = FP.get()
                ts("dve", re_[:, N], ps[:, N], 1.0 / D, NORM_EPS, ALU.mult, ALU.add)
                PS.put(ps)
                act(re_[:, N], re_[:, N], AF.Sqrt)
                recip(re_[:, N], re_[:, N])
                for c in range(8):
                    stt(E[c][:, N], E[c][:, N], cc("ple_g", c), re_[:, N], ALU.mult, ALU.mult)
                FP.put(re_)
                rms_norm(X, "pleg_g", H)
                for ch in range(2):
                    def consume_g(m, ps, ch=ch):
                        c = ch * 4 + m
                        gt = FP.get()
                        act(gt[:, N], ps[:, N], AF.Sigmoid)
                        tt("pool", gt[:, N], gt[:, N], E[c][:, N], ALU.mult)
                        tt("dve", X[c][:, N], X[c][:, N], gt[:, N], ALU.add)
                        FP.put(gt)
                    project(H, consume_g)
                for e_ in E:
                    FP.put(e_)

            except _Stop:
                pass
            if stop is None:
                rms_norm(X, "fin_g", H)
            par = bi % 2
            extra = out_grp_prev[par] or ()
            grp = []
            for c in range(8):
                grp.append(dma("sp" if 's' in _DBG else "act", out_d[c * 128:c * 128 + 128, t0:t0 + TB], H[c][:, N], f"out{par}",
                               reads=[H[c][:, N]], extra=extra))
            S.dma_groups.append(grp)
            out_grp_prev[par] = grp

        fin = []
        for par in (0, 1):
            if out_grp_prev[par]:
                fin += out_grp_prev[par]
        S.emit("act", None, extra_deps=fin)
        S.emit("sp", None, extra_deps=[o for o in S.ops if o.is_dma and o.sem == "const"])

        S.finalize(sems)
        with nc.Block() as block:
            if S.streams["pe"]:
                @block.tensor
                def _(e):
                    S.replay("pe", e, sems)

            if S.streams["act"]:
                @block.scalar
                def _(e):
                    S.replay("act", e, sems)

            if S.streams["dve"]:
                @block.vector
                def _(e):
                    S.replay("dve", e, sems)

            if S.streams["pool"]:
                @block.gpsimd
                def _(e):
                    S.replay("pool", e, sems)

            if S.streams["sp"]:
                @block.sync
                def _(e):
                    S.replay("sp", e, sems)
    return nc


def _prep_inputs(inp):
    x = np.asarray(inp["x"], np.float32)
    p = np.asarray(inp["p"], np.float32)[0]
    wpack = _pack_weights(np.asarray(inp["w_in"][0], np.float32), np.asarray(inp["w_out"][0], np.float32),
                          np.asarray(inp["ffn_w_up"][0], np.float32), np.asarray(inp["ffn_w_down"][0], np.float32),
                          np.asarray(inp["ple_w_proj"][0], np.float32), np.asarray(inp["ple_w_gate"][0], np.float32))
    cpack = _pack_consts(inp)
    maps = []
    for i in range(8):
        xs = x[2 * i:2 * i + 2].reshape(TOK, D)
        ps = p[2 * i:2 * i + 2].reshape(TOK, 256)
        maps.append({"xT": np.ascontiguousarray(xs.T), "pT": np.ascontiguousarray(ps.T),
                     "wpack": wpack, "cpack": cpack})
    return maps


def kernel(**inputs):
    maps = _prep_inputs(inputs)
    nc = build_nc()
    res = run_bass_kernel_spmd(nc, maps, core_ids=list(range(8)))
    out = np.empty((16, 2048, D), np.float32)
    for i in range(8):
        o = np.asarray(res.results[i]["outT"], np.float32)
        out[2 * i:2 * i + 2] = o.T.reshape(2, 2048, D)
    return out
```

```python
import math
import os
_DBG = os.environ.get('DBGSKIP', '')
from collections import deque
from contextlib import ExitStack

import numpy as np
import concourse.bass as bass
import concourse.mybir as mybir
from concourse.bass_utils import run_bass_kernel_spmd

F32 = mybir.dt.float32
F32R = mybir.dt.float32r
BF16 = mybir.dt.bfloat16
AF = mybir.ActivationFunctionType
ALU = mybir.AluOpType

D = 1024
TB = 512
NBLK = 8
TOK = NBLK * TB
DFF = 2816
NFF = 22
SLABW = 4096
NORM_EPS = 1e-6
GN_EPS = 64e-5


def _vec_cols(a):
    a = np.asarray(a, np.float32).reshape(-1)
    return a.reshape(-1, 128).T


def _slab(W, cols):
    K = W.shape[0]
    nk = K // 128
    sub = W[:, cols].reshape(nk, 128, len(cols)).transpose(1, 0, 2).reshape(128, nk * len(cols))
    out = np.zeros((128, SLABW), np.float32)
    out[:, : sub.shape[1]] = sub
    return out


def _r(a, b):
    return list(range(a, b))


def _slab_plan():
    plan = []
    for j in range(4):
        plan.append(("conv", 8, 384))
    plan.append(("lora", 8, 256))
    for hp in range(4):
        plan.append(("rwkv", 8, 384))
    for ch in range(2):
        plan.append(("wout", 8, 512))
    for hf in range(2):
        for s in range(5):
            plan.append(("ffup", 8, 512))
        plan.append(("ffup", 8, 256))
        for ch in range(2):
            plan.append(("ffdn", 6, 512))
            plan.append(("ffdn", 5, 512))
    for ch in range(2):
        plan.append(("plep", 2, 512))
    for ch in range(2):
        plan.append(("pleg", 8, 512))
    return plan


def _pack_weights(w_in, w_out, ffn_w_up, ffn_w_down, ple_w_proj, ple_w_gate):
    slabs = []
    for j in range(4):
        slabs.append(_slab(w_in, _r(j * 128, j * 128 + 128) + _r(512 + j * 128, 512 + j * 128 + 128)
                           + _r(1024 + j * 128, 1024 + j * 128 + 128)))
    slabs.append(_slab(w_in, _r(1536 + 1536, 1536 + 1792)))
    for hp in range(4):
        b = 1536
        slabs.append(_slab(w_in, _r(b + hp * 128, b + hp * 128 + 128) + _r(b + 512 + hp * 128, b + 512 + hp * 128 + 128)
                           + _r(b + 1024 + hp * 128, b + 1024 + hp * 128 + 128)))
    for ch in range(2):
        slabs.append(_slab(w_out, _r(ch * 512, ch * 512 + 512)))
    for hf in range(2):
        for s in range(6):
            js = [11 * hf + 2 * s, 11 * hf + 2 * s + 1] if s < 5 else [11 * hf + 10]
            cols = []
            for j in js:
                cols += _r(j * 128, j * 128 + 128) + _r(DFF + j * 128, DFF + j * 128 + 128)
            slabs.append(_slab(ffn_w_up, cols))
        for ch in range(2):
            for (ka, kb) in ((0, 6), (6, 11)):
                k0 = (11 * hf + ka) * 128
                k1 = (11 * hf + kb) * 128
                slabs.append(_slab(ffn_w_down[k0:k1], _r(ch * 512, ch * 512 + 512)))
    for ch in range(2):
        slabs.append(_slab(ple_w_proj, _r(ch * 512, ch * 512 + 512)))
    for ch in range(2):
        slabs.append(_slab(ple_w_gate, _r(ch * 512, ch * 512 + 512)))
    return np.stack(slabs, 0)


def _const_layout():
    items = [("ident", 128), ("ones", 128), ("BO", 128), ("mask1", 512), ("mask3", 128),
             ("reset", 512), ("I64x", 64), ("LW", 512), ("GU", 512),
             ("mix_g", 8), ("conv_w", 12), ("mu", 14), ("w0", 4), ("a0", 4), ("k_k", 4), ("k_a", 4),
             ("r_k", 4), ("gn_w", 4), ("gn_b", 4), ("ffn_g", 8), ("fcw", 132), ("fcb", 44),
             ("ple_g", 8), ("pleg_g", 8), ("fin_g", 8), ("epsn", 1), ("epsg", 1), ("b20", 1)]
    off = {}
    o = 0
    for n, w in items:
        off[n] = (o, w)
        o += w
    return off, o


def _pack_consts(inp):
    off, ncol = _const_layout()
    c = np.zeros((128, ncol), np.float32)

    def put(name, arr):
        o, w = off[name]
        assert arr.shape == (128, w), (name, arr.shape, w)
        c[:, o:o + w] = arr

    i = np.arange(128)
    put("ident", np.eye(128, dtype=np.float32))
    put("ones", np.ones((128, 128), np.float32))
    put("BO", (i[:, None] // 64 == i[None, :] // 64).astype(np.float32))
    same = (i[:, None] // 64 == i[None, :] // 64)
    su = (same & (i[:, None] < i[None, :])).astype(np.float32)
    sl = (same & (i[:, None] > i[None, :])).astype(np.float32)
    u = (same & (i[:, None] <= i[None, :])).astype(np.float32)
    put("mask1", np.concatenate([su, sl, su, u], 1))
    put("mask3", u)
    rs = np.ones((128, 512), np.float32)
    rs[:, 0::64] = 0.0
    put("reset", rs)
    put("I64x", (i[:, None] % 64 == np.arange(64)[None, :]).astype(np.float32))
    lw = np.zeros((128, 512), np.float32)
    lw[0:64] = inp["rwkv_w_up"][0]
    lw[64:128] = inp["rwkv_a_up"][0]
    put("LW", lw)
    put("GU", np.asarray(inp["rwkv_g_up"][0], np.float32))
    put("mix_g", _vec_cols(inp["mix_norm_g"][0]))
    put("conv_w", np.concatenate([_vec_cols(inp["conv_mix_w"][0][t]) for t in range(3)], 1))
    put("mu", _vec_cols(inp["rwkv_mu"][0]))
    put("w0", _vec_cols(inp["rwkv_w0"][0]))
    put("a0", _vec_cols(inp["rwkv_a0"][0]))
    put("k_k", _vec_cols(inp["rwkv_k_k"][0]))
    put("k_a", _vec_cols(inp["rwkv_k_a"][0]))
    put("r_k", _vec_cols(inp["rwkv_r_k"][0]))
    put("gn_w", _vec_cols(inp["rwkv_gn_w"][0]))
    put("gn_b", _vec_cols(inp["rwkv_gn_b"][0]))
    put("ffn_g", _vec_cols(inp["ffn_norm_g"][0]))
    put("fcw", np.concatenate([_vec_cols(inp["ffn_conv_w"][0][t]) for t in range(3)], 1))
    put("fcb", _vec_cols(inp["ffn_conv_b"][0]))
    put("ple_g", _vec_cols(inp["ple_norm_g"][0]))
    put("pleg_g", _vec_cols(inp["ple_gate_norm_g"][0]))
    put("fin_g", _vec_cols(inp["final_norm_g"]))
    put("epsn", np.full((128, 1), NORM_EPS, np.float32))
    put("epsg", np.full((128, 1), GN_EPS, np.float32))
    put("b20", np.full((128, 1), 20.0 * math.log(2.0), np.float32))
    return c


class TL:
    def __init__(self, h, name):
        self.h = h
        self.name = name
        self.last_w = None
        self.readers = []

    def __getitem__(self, k):
        return V(self, self.h[k])


class V:
    def __init__(self, tl, ap):
        self.tl = tl
        self.ap = ap

    def __getitem__(self, k):
        return V(self.tl, self.ap[k])

    def bc(self, dt):
        return V(self.tl, self.ap.bitcast(dt))


class Op:
    __slots__ = ("eng", "fn", "deps", "signal", "sem", "tick", "is_dma", "waits", "idx")


class Sched:
    ENG = ("pe", "act", "dve", "pool", "sp")

    def __init__(self):
        self.streams = {e: [] for e in self.ENG}
        self.ops = []
        self.dma_groups = []

    def emit(self, eng, fn, reads=(), writes=(), dma_sem=None, extra_deps=()):
        op = Op()
        op.eng = eng
        op.fn = fn
        op.signal = dma_sem is not None
        op.sem = dma_sem
        op.tick = None
        op.is_dma = dma_sem is not None
        op.waits = []
        deps = set(extra_deps)
        rt = []
        for v in reads:
            t = v.tl if isinstance(v, V) else (v if isinstance(v, TL) else None)
            if t is not None and t not in rt:
                rt.append(t)
        wt = []
        for v in writes:
            t = v.tl if isinstance(v, V) else v
            if t not in wt:
                wt.append(t)
        for t in rt:
            if t.last_w is not None:
                deps.add(t.last_w)
        for t in wt:
            if t.last_w is not None:
                deps.add(t.last_w)
            for r in t.readers:
                deps.add(r)
        for t in rt:
            t.readers.append(op)
        for t in wt:
            t.last_w = op
            t.readers = []
        deps.discard(op)
        op.deps = deps
        op.idx = len(self.ops)
        self.ops.append(op)
        self.streams[eng].append(op)
        return op

    def finalize(self, sems):
        for op in self.ops:
            for d in list(op.deps):
                if d.is_dma:
                    continue
                if d.eng == "pe" and op.eng == "pe" and not op.is_dma:
                    op.deps.discard(d)
                    continue
                d.signal = True
        cnt = {}
        for op in self.ops:
            if op.is_dma:
                cnt[op.sem] = cnt.get(op.sem, 0) + 16
                op.tick = cnt[op.sem]
        for grp in self.dma_groups:
            m = max(o.tick for o in grp)
            gs = set(grp)
            for o in grp:
                o.tick = m
                o.deps -= gs
        for e in self.ENG:
            c = 0
            for op in self.streams[e]:
                if op.is_dma:
                    continue
                if op.signal:
                    c += 1
                    op.tick = c
                    op.sem = e
        for e in self.ENG:
            waited = {}
            for op in self.streams[e]:
                need = {}
                for d in op.deps:
                    need[d.sem] = max(need.get(d.sem, 0), d.tick)
                for s, v in need.items():
                    if waited.get(s, 0) >= v:
                        continue
                    waited[s] = v
                    op.waits.append((s, v))

    def replay(self, eng, e, sems):
        for op in self.streams[eng]:
            for s, v in op.waits:
                e.wait_ge(sems[s], v)
            if op.fn is None:
                continue
            ins = op.fn(e)
            if op.signal:
                ins.then_inc(sems[op.sem], 16 if op.is_dma else 1)


class Pool_:
    def __init__(self, tiles):
        self.q = deque(tiles)

    def get(self):
        return self.q.popleft()

    def put(self, *ts):
        for t in ts:
            self.q.append(t)


class _Stop(Exception):
    pass


def build_nc(nblk=NBLK, taps=(), stop=None):
    nc = bass.Bass("TRN2", target_bir_lowering=False)
    nc.dge_precook = False
    plan = _slab_plan()
    nslab = len(plan)
    coff, ncol = _const_layout()
    xT_d = nc.dram_tensor("xT", [D, TOK], F32, kind="ExternalInput").ap()
    pT_d = nc.dram_tensor("pT", [256, TOK], F32R, kind="ExternalInput").ap()
    wp_d = nc.dram_tensor("wpack", [nslab, 128, SLABW], F32R, kind="ExternalInput").ap()
    cp_d = nc.dram_tensor("cpack", [128, ncol], F32, kind="ExternalInput").ap()
    out_d = nc.dram_tensor("outT", [D, TOK], F32, kind="ExternalOutput").ap()
    tap_d = {}
    for name, shape in taps:
        tap_d[name] = nc.dram_tensor("tap_" + name, list(shape), F32, kind="ExternalOutput").ap()

    S = Sched()
    NFP = 21
    NRP = 12
    NBP = 20
    NBS = 17
    with ExitStack() as st:
        def sb(name, shape, dt):
            return st.enter_context(nc.sbuf_tensor(name, shape, dt))

        cst_h = sb("cst", [128, ncol], F32)
        cst = TL(cst_h, "cst")
        xall = sb("xall", [128, 8, TB], F32)
        X = [TL(xall[:, c, :], f"x{c}") for c in range(8)]
        hall = sb("hall", [128, 8, TB], F32)
        H = [TL(hall[:, c, :], f"h{c}") for c in range(8)]
        pall = sb("pall", [128, 2, TB], F32R)
        PT = [TL(pall[:, c, :], f"p{c}") for c in range(2)]
        ring_h = sb("ring", [128, 3, SLABW], F32R)
        RING = [TL(ring_h[:, i, :], f"ring{i}") for i in range(3)]
        fp_h = sb("fpool", [128, NFP, 516], F32)
        FP = Pool_([TL(fp_h[:, i, :], f"fp{i}") for i in range(NFP)])
        rp_h = sb("rpool", [128, NRP, TB], F32R)
        RP = Pool_([TL(rp_h[:, i, :], f"rp{i}") for i in range(NRP)])
        bp_h = sb("bpool", [128, NBP, 512], BF16)
        BP = Pool_([TL(bp_h[:, i, :], f"bp{i}") for i in range(NBP)])
        bs_h = sb("bspool", [128, NBS, 256], BF16)
        BPS = Pool_([TL(bs_h[:, i, :], f"bs{i}") for i in range(NBS)])
        tm_h = sb("tm", [128, 2048], BF16)
        TM = TL(tm_h, "tm")
        cb_h = sb("cbf", [128, 384], BF16)
        CB = TL(cb_h, "cbf")
        misc_h = sb("misc", [128, 16], F32)
        MISC = TL(misc_h, "misc")
        hc_h = sb("hconv", [128, 4, 2], F32)
        HCONV = TL(hc_h, "hconv")
        hr_h = sb("hrw", [128, 14], F32)
        HRW = TL(hr_h, "hrw")
        hf_h = sb("hffn", [128, 44, 2], F32)
        HFFN = TL(hf_h, "hffn")
        stw_h = sb("stw", [128, 9 * 128], F32)
        STW = TL(stw_h, "stw")
        car_h = sb("carry", [128, 4, 128], F32)
        CARRY = [TL(car_h[:, hp, :], f"carry{hp}") for hp in range(4)]
        ps_h = [st.enter_context(nc.psum_tensor(f"ps{i}", [128, 512], F32)) for i in range(6)]
        PS = Pool_([TL(ps_h[i], f"ps{i}") for i in range(6)])
        psb_h = [st.enter_context(nc.psum_tensor(f"psb{i}", [128, 1024], BF16)) for i in range(2)]
        PSB = Pool_([TL(psb_h[i], f"psb{i}") for i in range(2)])

        sem_names = ["pe", "act", "dve", "pool", "sp", "ring0", "ring1", "ring2", "x", "p", "out0", "out1", "const"]
        sems = {n: st.enter_context(nc.semaphore(n)) for n in sem_names}

        def cc(name, j=0, w=1):
            o, _ = coff[name]
            return cst[:, o + j:o + j + w]

        def mm(out, lhsT, rhs, start=True, stop=True):
            S.emit("pe", lambda e: e.matmul(out.ap, lhsT.ap, rhs.ap, start=start, stop=stop),
                   [lhsT, rhs], [out])

        def tr(out, in_, ident):
            S.emit("pe", lambda e: e.transpose(out.ap, in_.ap, ident.ap), [in_, ident], [out])

        def act(out, in_, func, bias=None, scale=None):
            kw = {}
            rd = [in_]
            if bias is not None:
                kw["bias"] = bias.ap if isinstance(bias, V) else bias
                rd.append(bias)
            if scale is not None:
                kw["scale"] = scale.ap if isinstance(scale, V) else scale
                rd.append(scale)
            S.emit("act", lambda e: e.activation(out.ap, in_.ap, func, **kw), rd, [out])

        def tt(eng, out, a, b, op):
            S.emit(eng, lambda e: e.tensor_tensor(out.ap, a.ap, b.ap, op), [a, b], [out])

        def ts(eng, out, a, s1, s2, op0, op1=None):
            rd = [a, s1, s2]
            s1a = s1.ap if isinstance(s1, V) else s1
            s2a = s2.ap if isinstance(s2, V) else s2
            if op1 is None:
                S.emit(eng, lambda e: e.tensor_scalar(out.ap, a.ap, s1a, None, op0), rd, [out])
            else:
                S.emit(eng, lambda e: e.tensor_scalar(out.ap, a.ap, s1a, s2a, op0, op1), rd, [out])

        def stt(out, a, s, b, op0, op1):
            sa = s.ap if isinstance(s, V) else s
            S.emit("dve", lambda e: e.scalar_tensor_tensor(out.ap, a.ap, sa, b.ap, op0, op1), [a, s, b], [out])

        def cp(eng, out, a):
            if eng == "act":
                S.emit("act", lambda e: e.activation(out.ap, a.ap, AF.Copy), [a], [out])
            else:
                S.emit(eng, lambda e: e.tensor_copy(out.ap, a.ap), [a], [out])

        def recip(out, a):
            S.emit("dve", lambda e: e.reciprocal(out.ap, a.ap), [a], [out])

        def memset(eng, out, val):
            S.emit(eng, lambda e: e.memset(out.ap, val), [], [out])

        def dma(eng, out, in_, sem, reads=(), writes=(), extra=()):
            oa = out.ap if isinstance(out, V) else out
            ia = in_.ap if isinstance(in_, V) else in_
            return S.emit(eng, lambda e: e.dma_start(out=oa, in_=ia), reads, writes, dma_sem=sem, extra_deps=extra)

        def tap(name, v):
            if name in tap_d:
                dma("sp", tap_d[name], v, "const", reads=[v])

        N = slice(0, TB)

        def ck(k):
            if stop is not None and stop == k:
                raise _Stop()

        g = []
        half = ncol // 2
        g.append(dma("sp", cst[:, 0:half], cp_d[:, 0:half], "const", writes=[cst[:, :]]))
        g.append(dma("sp", cst[:, half:ncol], cp_d[:, half:ncol], "const", writes=[cst[:, :]]))
        S.dma_groups.append(g)
        cp("dve", CB[:, 0:128], cc("ident", 0, 128))
        cp("dve", CB[:, 128:256], cc("ident", 0, 128))
        cp("dve", CB[:, 256:384], cc("ident", 0, 128))
        IDB = CB[:, 0:128]
        I2 = CB[:, 128:384]
        ts("dve", MISC[:, 0:4], cc("k_a", 0, 4), -1.0, 1.0, ALU.mult, ALU.add)
        ones_h = sb("onesr", [128, 128], F32R)
        ONESR = TL(ones_h, "onesr")
        if 'o' not in _DBG:
            cp("dve", ONESR[:, :], cc("ones", 0, 128))
        ONES_R = ONESR[:, :]

        slab_ctr = [0]

        def load_slab(blk_slab_idx):
            kind, nk, ncols = plan[blk_slab_idx]
            i = slab_ctr[0]
            slab_ctr[0] += 1
            slot = RING[i % 3]
            w = nk * ncols
            h2 = w // 2
            grp = [dma("sp", slot[:, 0:h2], wp_d[blk_slab_idx, :, 0:h2], f"ring{i % 3}", writes=[slot]),
                   dma("sp", slot[:, h2:w], wp_d[blk_slab_idx, :, h2:w], f"ring{i % 3}", writes=[slot])]
            S.dma_groups.append(grp)
            return slot, nk, ncols

        def wv(slot, ncols, k, m):
            return slot[:, k * ncols + m * 128:k * ncols + m * 128 + 128]

        bslab = [0]

        def next_slab():
            r = load_slab(bslab[0])
            bslab[0] += 1
            return r

        def project(rhs, consume):
            slot, nk, ncols = next_slab()
            for m in range(ncols // 128):
                ps = PS.get()
                for k in range(nk):
                    mm(ps[:, N], wv(slot, ncols, k, m), (rhs[k][:, N].bc(F32R) if rhs is H else rhs[k][:, N]), start=(k == 0), stop=(k == nk - 1))
                consume(m, ps)
                PS.put(ps)

        def rms_norm(xs, gname, outs):
            ps = PS.get()
            for c in range(8):
                sq = RP.get()
                act(sq[:, N], xs[c][:, N], AF.Square)
                mm(ps[:, N], ONES_R, sq[:, N], start=(c == 0), stop=(c == 7))
                RP.put(sq)
            r = FP.get()
            act(r[:, N], ps[:, N], AF.Ln, bias=cc("epsn", 0), scale=1.0 / D)
            PS.put(ps)
            act(r[:, N], r[:, N], AF.Exp, scale=-0.5)
            for c in range(8):
                stt(outs[c][:, N].bc(F32R), xs[c][:, N], cc(gname, c), r[:, N], ALU.mult, ALU.mult)
            FP.put(r)

        out_grp_prev = {0: None, 1: None}

        for bi in range(nblk):
            seq_start = (bi % 4 == 0)
            t0 = bi * TB
            bslab[0] = 0
            S.dma_groups.append([dma("sp" if 'x' in _DBG else "act", X[c][:, N], xT_d[c * 128:c * 128 + 128, t0:t0 + TB], "x", writes=[X[c]])
                                 for c in range(8)])
            if 'p' not in _DBG:
                S.dma_groups.append([dma("act", PT[c][:, N], pT_d[c * 128:c * 128 + 128, t0:t0 + TB], "p", writes=[PT[c]])
                                     for c in range(2)])
            if seq_start and 'm' not in _DBG:
                memset("dve", HCONV[:, :, :], 0.0)
                memset("dve", HRW[:, :], 0.0)
                memset("dve", HFFN[:, :, :], 0.0)
                for hp in range(4):
                    memset("dve", CARRY[hp][:, :], 0.0)

            try:
                ck(0)
                rms_norm(X, "mix_g", H)
                if bi == 0:
                    tap("h0", H[0][:, N])
                ck(1)
                Y = [RP.get() for _ in range(8)]

                for j in range(4):
                    hold = {}

                    def consume(m, ps, j=j, hold=hold):
                        if m == 0:
                            hold["xin"] = FP.get()
                            cp("act", hold["xin"][:, N], ps[:, N])
                        elif m == 1:
                            hold["b"] = FP.get()
                            cp("act", hold["b"][:, N], ps[:, N])
                        else:
                            cx = FP.get()
                            tt("dve", cx[:, 2:2 + TB], ps[:, N], hold["xin"][:, N], ALU.mult)
                            cp("dve", cx[:, 0:2], HCONV[:, j, :])
                            cp("dve", HCONV[:, j, :], cx[:, TB:TB + 2])
                            acc = FP.get()
                            act(acc[:, N], cx[:, 2:2 + TB], AF.Identity, scale=cc("conv_w", 8 + j))
                            stt(acc[:, N], cx[:, 1:1 + TB], cc("conv_w", 4 + j), acc[:, N], ALU.mult, ALU.add)
                            stt(acc[:, N], cx[:, 0:TB], cc("conv_w", j), acc[:, N], ALU.mult, ALU.add)
                            tt("dve", Y[j][:, N], hold["b"][:, N], acc[:, N], ALU.mult)
                            FP.put(cx, acc, hold["xin"], hold["b"])

                    project(H, consume)
                if bi == 0:
                    tap("yc0", Y[0][:, N])
                ck(2)

                def lerp_evac(ps, cid):
                    zs = FP.get()
                    cp("act", zs[:, 1:1 + TB], ps[:, N])
                    cp("dve", zs[:, 0:1], HRW[:, cid:cid + 1])
                    cp("dve", HRW[:, cid:cid + 1], zs[:, TB:TB + 1])
                    d = FP.get()
                    tt("pool", d[:, N], zs[:, 0:TB], zs[:, 1:1 + TB], ALU.subtract)
                    zl = FP.get()
                    stt(zl[:, N], d[:, N], cc("mu", cid), zs[:, 1:1 + TB], ALU.mult, ALU.add)
                    FP.put(zs, d)
                    return zl

                lora = {}

                def consume_lora(m, ps):
                    lora[m] = lerp_evac(ps, 12 + m)

                project(H, consume_lora)
                z12, z13 = lora[0], lora[1]
                tw = FP.get()
                act(tw[0:64, N], z12[0:64, N], AF.Tanh)
                sg = FP.get()
                act(sg[:, N], z13[:, N], AF.Sigmoid)
                FP.put(z13)
                ck(3)

                for hp in range(4):
                    zz = {}

                    def consume_r(m, ps, hp=hp, zz=zz):
                        zz[m] = lerp_evac(ps, m * 4 + hp)

                    project(H, consume_r)
                    zr, zk, zv = zz[0], zz[1], zz[2]
                    LWo = coff["LW"][0]
                    GUo = coff["GU"][0]
                    ps = PS.get()
                    mm(ps[:, N], cst[0:64, LWo + hp * 128:LWo + hp * 128 + 128], tw[0:64, N])
                    lw = FP.get()
                    act(lw[:, N], ps[:, N], AF.Sigmoid, bias=cc("w0", hp))
                    PS.put(ps)
                    ts("dve", lw[:, N], lw[:, N], -math.exp(-0.5), None, ALU.mult)
                    logP = FP.get()
                    S.emit("dve", lambda e, o=logP[:, N], d0=cc("reset", 0, 512), d1=lw[:, N]:
                           e.tensor_tensor_scan(o.ap, d0.ap, d1.ap, 0.0, ALU.mult, ALU.add),
                           [cc("reset", 0, 512), lw[:, N]], [logP[:, N]])
                    ps = PS.get()
                    mm(ps[:, N], cst[64:128, LWo + hp * 128:LWo + hp * 128 + 128], z12[64:128, N])
                    a_ = FP.get()
                    act(a_[:, N], ps[:, N], AF.Sigmoid, bias=cc("a0", hp))
                    PS.put(ps)
                    ps = PS.get()
                    mm(ps[:, N], cst[:, GUo + hp * 128:GUo + hp * 128 + 128], sg[:, N])
                    g_ = FP.get()
                    cp("act", g_[:, N], ps[:, N])
                    PS.put(ps)
                    kq = FP.get()
                    act(kq[:, N], zk[:, N], AF.Identity, scale=cc("k_k", hp))
                    sq = FP.get()
                    act(sq[:, N], kq[:, N], AF.Square)
                    ps = PS.get()
                    mm(ps[:, N], cc("BO", 0, 128), sq[:, N])
                    ts("dve", sq[:, N], ps[:, N], 1e-24, None, ALU.max)
                    PS.put(ps)
                    act(sq[:, N], sq[:, N], AF.Ln, scale=float(2.0 ** 40))
                    act(sq[:, N], sq[:, N], AF.Exp, bias=cc("b20", 0), scale=-0.5)
                    kk = FP.get()
                    tt("dve", kk[:, N], kq[:, N], sq[:, N], ALU.mult)
                    FP.put(kq, sq)
                    f_ = FP.get()
                    ts("dve", f_[:, N], a_[:, N], cc("k_a", hp), MISC[:, hp:hp + 1], ALU.mult, ALU.add)
                    kp = FP.get()
                    tt("pool", kp[:, N], zk[:, N], f_[:, N], ALU.mult)
                    FP.put(f_, zk)
                    ba = FP.get()
                    tt("pool", ba[:, N], kk[:, N], a_[:, N], ALU.mult)
                    FP.put(a_)
                    e1 = FP.get()
                    act(e1[:, N], logP[:, N], AF.Exp)
                    e2 = FP.get()
                    act(e2[:, N], logP[:, N], AF.Exp, scale=-1.0)
                    e3 = FP.get()
                    tt("pool", e3[:, N], logP[:, N], lw[:, N], ALU.subtract)
                    act(e3[:, N], e3[:, N], AF.Exp)
                    e4 = FP.get()
                    for c in range(8):
                        cs = slice(c * 64, c * 64 + 64)
                        act(e4[:, cs], logP[:, cs], AF.Exp, bias=logP[:, c * 64 + 63:c * 64 + 64], scale=-1.0)
                    FP.put(lw, logP)
                    Rt = BP.get()
                    tt("dve", Rt[:, N], zr[:, N], e1[:, N], ALU.mult)
                    Bt = BP.get()
                    tt("dve", Bt[:, N], ba[:, N], e2[:, N], ALU.mult)
                    Kt = BP.get()
                    tt("dve", Kt[:, N], kp[:, N], e2[:, N], ALU.mult)
                    At = BP.get()
                    stt(At[:, N], kk[:, N], -1.0, e3[:, N], ALU.mult, ALU.mult)
                    Bh = BP.get()
                    tt("dve", Bh[:, N], ba[:, N], e4[:, N], ALU.mult)
                    Kh = BP.get()
                    tt("dve", Kh[:, N], kp[:, N], e4[:, N], ALU.mult)
                    vb = BP.get()
                    cp("act", vb[:, N], zv[:, N])
                    FP.put(e2, e3, e4, ba, kk)
                    rk = FP.get()
                    stt(rk[:, N], zr[:, N], cc("r_k", hp), kp[:, N], ALU.mult, ALU.mult)
                    ps = PS.get()
                    mm(ps[:, N], cc("BO", 0, 128), rk[:, N])
                    bonus = FP.get()
                    tt("dve", bonus[:, N], ps[:, N], zv[:, N], ALU.mult)
                    PS.put(ps)
                    FP.put(rk, kp, zr, zv)
                    if bi == 0 and hp == 0:
                        tap("g0", g_[:, N])
                        tap("bonus0", bonus[:, N])
                        tap("e1", e1[:, N])
                    ck(4)

                    srcs = [At, Bh, Kh, vb]
                    for pr in range(2):
                        pb = PSB.get()
                        for t2 in range(2):
                            tti = pr * 2 + t2
                            for q in range(4):
                                tr(pb[:, t2 * 512 + q * 128:t2 * 512 + q * 128 + 128],
                                   srcs[q][:, tti * 128:tti * 128 + 128], IDB)
                        cp("act", TM[:, pr * 1024:pr * 1024 + 1024], pb[:, 0:1024])
                        PSB.put(pb)
                    BP.put(Bh, Kh, vb)
                    ck(5)

                    def tm(tti, q, h, rows=slice(0, 128)):
                        o = tti * 512 + q * 128 + h * 64
                        return TM[rows, o:o + 64]

                    def tmf(tti, q, rows=slice(0, 128)):
                        o = tti * 512 + q * 128
                        return TM[rows, o:o + 128]

                    RH = FP.get()
                    AcTs = [FP.get(), FP.get()]
                    Dts = [FP.get(), FP.get()]
                    for d_ in Dts:
                        memset("dve", d_[:, N], 0.0)

                    def acv(c):
                        return AcTs[c // 4][:, (c % 4) * 128:(c % 4) * 128 + 128]

                    def dv(c, rows=slice(0, 128), cols=slice(0, 128)):
                        o = (c % 4) * 128
                        return Dts[c // 4][rows, o + cols.start:o + cols.stop]

                    keep = []
                    U = [dict() for _ in range(4)]
                    toks = [slice(t_ * 128, t_ * 128 + 128) for t_ in range(4)]
                    for tti in range(4):
                        tok = toks[tti]
                        bH = [PS.get(), PS.get()]
                        bK = [PS.get(), PS.get()]
                        W = []
                        MK = BPS.get()
                        for h in range(2):
                            rows = slice(64 * h, 64 * h + 64)
                            mm(bH[h][:, 0:128], Bt[rows, tok], At[rows, tok])
                            mm(bH[h][:, 128:256], At[rows, tok], Bt[rows, tok])
                            mm(bH[h][:, 256:384], Kt[rows, tok], At[rows, tok])
                            mm(bH[h][:, 384:512], Bt[rows, tok], Rt[rows, tok])
                            mm(bK[h][:, 0:128], Kt[rows, tok], Rt[rows, tok])
                        for h in range(2):
                            Wh = BP.get()
                            tt("dve", Wh[:, N], bH[h][:, N], cc("mask1", 0, 512), ALU.mult)
                            tt("dve", MK[:, h * 128:h * 128 + 128], bK[h][:, 0:128], cc("mask3", 0, 128), ALU.mult)
                            W.append(Wh)
                        PS.put(*bH)
                        PS.put(*bK)
                        Tt = BPS.get()
                        for h in range(2):
                            tt("pool", Tt[:, h * 128:h * 128 + 128], W[h][:, 0:128], IDB, ALU.add)
                        U[tti].update(W=W, MK=MK, Tt=Tt, cur=None)
                    for lev in range(5):
                        for tti in range(4):
                            u = U[tti]
                            W, cur = u["W"], u["cur"]
                            bk = PS.get()
                            nxt = BP.get()
                            for h in range(2):
                                if cur is None:
                                    Np = W[h][:, 0:128]
                                    Lp = W[h][:, 128:256]
                                else:
                                    Np = cur[:, h * 128:h * 128 + 128]
                                    Lp = cur[:, 256 + h * 128:256 + h * 128 + 128]
                                if lev < 4:
                                    mm(bk[:, h * 128:h * 128 + 128], Lp, Np)
                                mm(bk[:, 256 + h * 128:256 + h * 128 + 128], Np, Lp)
                            if lev < 4:
                                cp("act", nxt[:, N], bk[:, N])
                            else:
                                cp("act", nxt[:, 256:512], bk[:, 256:512])
                            PS.put(bk)
                            if cur is not None:
                                BP.put(cur)
                            u["cur"] = nxt
                        for tti in range(4):
                            u = U[tti]
                            nxt, Tt = u["cur"], u["Tt"]
                            bT = PS.get()
                            for h in range(2):
                                mm(bT[:, h * 128:h * 128 + 128], nxt[:, 256 + h * 128:256 + h * 128 + 128],
                                   Tt[:, h * 128:h * 128 + 128], start=True, stop=False)
                                mm(bT[:, h * 128:h * 128 + 128], IDB, Tt[:, h * 128:h * 128 + 128], start=False, stop=True)
                            Tn = BPS.get()
                            cp("dve", Tn[:, 0:256], bT[:, 0:256])
                            PS.put(bT)
                            BPS.put(Tt)
                            u["Tt"] = Tn
                    for tti in range(4):
                        BP.put(U[tti]["cur"])
                    for tti in range(4):
                        u = U[tti]
                        bV = PS.get()
                        for h in range(2):
                            mm(bV[:, h * 64:h * 64 + 64], u["W"][h][:, 256:384], tm(tti, 3, h))
                        LV = BPS.get()
                        cp("act", LV[:, 0:128], bV[:, 0:128])
                        PS.put(bV)
                        u["LV"] = LV
                    for tti in range(4):
                        u = U[tti]
                        Tt, LV = u["Tt"], u["LV"]
                        bY = PS.get()
                        for h in range(2):
                            mm(bY[:, h * 64:h * 64 + 64], Tt[:, h * 128:h * 128 + 128], tm(tti, 0, h))
                            mm(bY[:, 128 + h * 64:128 + h * 64 + 64], Tt[:, h * 128:h * 128 + 128], LV[:, h * 64:h * 64 + 64])
                        Yb = BPS.get()
                        cp("dve", Yb[:, 0:256], bY[:, 0:256])
                        PS.put(bY)
                        BPS.put(LV, Tt)
                        u["Yb"] = Yb
                    for tti in range(4):
                        u = U[tti]
                        Yb, W, tok = u["Yb"], u["W"], toks[tti]
                        bR = PS.get()
                        for h in range(2):
                            mm(bR[64 * h:64 * h + 64, 0:128], Yb[:, h * 64:h * 64 + 64], W[h][:, 384:512])
                        tt("dve", RH[:, tok], bR[:, 0:128], Rt[:, tok], ALU.add)
                        PS.put(bR)
                    for tti in range(4):
                        u = U[tti]
                        Yb = u["Yb"]
                        bAD = [PS.get(), PS.get()]
                        for c2 in range(2):
                            tr_ = slice(64 * c2, 64 * c2 + 64)
                            mm(bAD[c2][:, 0:128], Yb[tr_, 0:128], tmf(tti, 1, tr_))
                            for h in range(2):
                                mm(bAD[c2][:, 128 + h * 64:128 + h * 64 + 64], tmf(tti, 1, tr_),
                                   Yb[tr_, 128 + h * 64:128 + h * 64 + 64], start=True, stop=False)
                                mm(bAD[c2][:, 128 + h * 64:128 + h * 64 + 64], tmf(tti, 2, tr_), tm(tti, 3, h, tr_),
                                   start=False, stop=True)
                        for c2 in range(2):
                            c = tti * 2 + c2
                            tmpA = FP.get()
                            cp("act", tmpA[:, 0:256], bAD[c2][:, 0:256])
                            tt("pool", tmpA[:, 0:128], tmpA[:, 0:128], cc("BO", 0, 128), ALU.mult)
                            stt(acv(c), cc("ident", 0, 128), e1[:, c * 64 + 63:c * 64 + 64], tmpA[:, 0:128], ALU.mult, ALU.add)
                            for h in range(2):
                                hr = slice(64 * h, 64 * h + 64)
                                cp("pool", dv(c, hr, slice(64 * h, 64 * h + 64)), tmpA[hr, 128 + h * 64:128 + h * 64 + 64])
                            FP.put(tmpA)
                        PS.put(*bAD)
                        keep.append((Yb, u["W"][0], u["W"][1], u["MK"]))
                    BP.put(At, Bt, Kt, Rt)
                    FP.put(e1)
                    ck(6)
                    cp("dve", STW[:, 0:128], CARRY[hp][:, :])
                    for c in range(8):
                        bS = PS.get()
                        mm(bS[:, 0:128], acv(c), STW[:, c * 128:c * 128 + 128])
                        tt("dve", STW[:, (c + 1) * 128:(c + 1) * 128 + 128], bS[:, 0:128], dv(c), ALU.add)
                        PS.put(bS)
                    cp("dve", CARRY[hp][:, :], STW[:, 1024:1152])
                    bO = PS.get()
                    for tti in range(4):
                        tok = slice(tti * 128, tti * 128 + 128)
                        Yb, W0, W1, MK = keep[tti]
                        Wl = [W0, W1]
                        for h in range(2):
                            hr = slice(64 * h, 64 * h + 64)
                            S.emit("pe", lambda e, o=bO[hr, tok], l=Yb[:, 128 + h * 64:128 + h * 64 + 64], r=Wl[h][:, 384:512]:
                                   e.matmul(o.ap, l.ap, r.ap, start=True, stop=False, skip_group_check=True),
                                   [Yb, Wl[h]], [bO])
                            S.emit("pe", lambda e, o=bO[hr, tok], l=tm(tti, 3, h), r=MK[:, h * 128:h * 128 + 128]:
                                   e.matmul(o.ap, l.ap, r.ap, start=False, stop=False, skip_group_check=True),
                                   [TM, MK], [bO])
                        for c2 in range(2):
                            c = tti * 2 + c2
                            cs = slice(c * 64, c * 64 + 64)
                            S.emit("pe", lambda e, o=bO[:, cs], l=STW[:, c * 128:c * 128 + 128], r=RH[:, cs], last=(c2 == 1):
                                   e.matmul(o.ap, l.ap, r.ap, start=False, stop=last, skip_group_check=True),
                                   [STW, RH], [bO])
                        BP.put(W0, W1)
                        BPS.put(Yb, MK)
                    osb = FP.get()
                    cp("act", osb[:, N], bO[:, N])
                    PS.put(bO)
                    FP.put(RH, *AcTs, *Dts)
                    ck(7)
                    if bi == 0 and hp == 0:
                        tap("o0", osb[:, N])
                    ps = PS.get()
                    mm(ps[:, N], cc("BO", 0, 128), osb[:, N])
                    oc = FP.get()
                    stt(oc[:, N], ps[:, N], -1.0 / 64, osb[:, N], ALU.mult, ALU.add)
                    PS.put(ps)
                    sq = FP.get()
                    act(sq[:, N], oc[:, N], AF.Square)
                    ps = PS.get()
                    mm(ps[:, N], cc("BO", 0, 128), sq[:, N])
                    act(sq[:, N], ps[:, N], AF.Ln, bias=cc("epsg", 0), scale=1.0 / 64)
                    PS.put(ps)
                    act(sq[:, N], sq[:, N], AF.Exp, scale=-0.5)
                    tt("dve", oc[:, N], oc[:, N], sq[:, N], ALU.mult)
                    ts("dve", oc[:, N], oc[:, N], cc("gn_w", hp), cc("gn_b", hp), ALU.mult, ALU.add)
                    tt("pool", oc[:, N], oc[:, N], bonus[:, N], ALU.add)
                    tt("dve", Y[4 + hp][:, N], oc[:, N], g_[:, N], ALU.mult)
                    FP.put(osb, oc, sq, bonus, g_)
                FP.put(tw, sg, z12)
                if bi == 0:
                    tap("yr0", Y[4][:, N])
                ck(8)

                for ch in range(2):
                    def consume_o(m, ps, ch=ch):
                        c = ch * 4 + m
                        tt("dve", X[c][:, N], ps[:, N], X[c][:, N], ALU.add)
                    project(Y, consume_o)
                RP.put(*Y)
                if bi == 0:
                    tap("x1", X[0][:, N])
                ck(9)

                rms_norm(X, "ffn_g", H)
                for hf in range(2):
                    FA = [RP.get() for _ in range(11)]
                    for s in range(6):
                        accs = {}

                        def consume_f(m, ps, s=s, accs=accs, hf=hf, FA=FA):
                            jl = 2 * s + m // 2
                            j = 11 * hf + jl
                            q = j if (m % 2 == 0) else 22 + j
                            us = FP.get()
                            cp("act", us[:, 2:2 + TB], ps[:, N])
                            cp("dve", us[:, 0:2], HFFN[:, q, :])
                            cp("dve", HFFN[:, q, :], us[:, TB:TB + 2])
                            acc = FP.get()
                            act(acc[:, N], us[:, 2:2 + TB], AF.Identity, bias=cc("fcb", q), scale=cc("fcw", 88 + q))
                            stt(acc[:, N], us[:, 1:1 + TB], cc("fcw", 44 + q), acc[:, N], ALU.mult, ALU.add)
                            stt(acc[:, N], us[:, 0:TB], cc("fcw", q), acc[:, N], ALU.mult, ALU.add)
                            FP.put(us)
                            if m % 2 == 0:
                                accs["g"] = acc
                            else:
                                gt = accs["g"]
                                sgm = FP.get()
                                act(sgm[:, N], gt[:, N], AF.Sigmoid)
                                tt("pool", sgm[:, N], sgm[:, N], gt[:, N], ALU.mult)
                                tt("dve", FA[jl][:, N], sgm[:, N], acc[:, N], ALU.mult)
                                FP.put(sgm, gt, acc)

                        project(H, consume_f)
                    for ch in range(2):
                        banks = [PS.get() for _ in range(4)]
                        for (ka, kb) in ((0, 6), (6, 11)):
                            slot, nk, ncols = next_slab()
                            for m in range(4):
                                for k in range(nk):
                                    jl = ka + k
                                    mm(banks[m][:, N], wv(slot, ncols, k, m), FA[jl][:, N],
                                       start=(jl == 0), stop=(jl == 10))
                        for m in range(4):
                            c = ch * 4 + m
                            tt("dve", X[c][:, N], banks[m][:, N], X[c][:, N], ALU.add)
                        PS.put(*banks)
                    RP.put(*FA)
                if bi == 0:
                    tap("x2", X[0][:, N])
                ck(10)

                E = []
                for ch in range(2):
                    def consume_e(m, ps, ch=ch):
                        e_ = FP.get()
                        cp("act", e_[:, N], ps[:, N])
                        E.append(e_)
                    project(PT, consume_e)
                ps = PS.get()
                for c in range(8):
                    sq = RP.get()
                    act(sq[:, N], E[c][:, N], AF.Square)
                    mm(ps[:, N], ONES_R, sq[:, N], start=(c == 0), stop=(c == 7))
                    RP.put(sq)
                re_ = FP.get()
                act(re_[:, N], ps[:, N], AF.Ln, bias=cc("epsn", 0), scale=1.0 / D)
                PS.put(ps)
                act(re_[:, N], re_[:, N], AF.Exp, scale=-0.5)
                for c in range(8):
                    stt(E[c][:, N], E[c][:, N], cc("ple_g", c), re_[:, N], ALU.mult, ALU.mult)
                FP.put(re_)
                rms_norm(X, "pleg_g", H)
                for ch in range(2):
                    def consume_g(m, ps, ch=ch):
                        c = ch * 4 + m
                        gt = FP.get()
                        act(gt[:, N], ps[:, N], AF.Sigmoid)
                        tt("pool", gt[:, N], gt[:, N], E[c][:, N], ALU.mult)
                        tt("dve", X[c][:, N], X[c][:, N], gt[:, N], ALU.add)
                        FP.put(gt)
                    project(H, consume_g)
                for e_ in E:
                    FP.put(e_)

            except _Stop:
                pass
            if stop is None:
                rms_norm(X, "fin_g", H)
            par = bi % 2
            extra = out_grp_prev[par] or ()
            grp = []
            for c in range(8):
                grp.append(dma("sp" if 's' in _DBG else "act", out_d[c * 128:c * 128 + 128, t0:t0 + TB], H[c][:, N], f"out{par}",
                               reads=[H[c][:, N]], extra=extra))
            S.dma_groups.append(grp)
            out_grp_prev[par] = grp

        fin = []
        for par in (0, 1):
            if out_grp_prev[par]:
                fin += out_grp_prev[par]
        S.emit("act", None, extra_deps=fin)
        S.emit("sp", None, extra_deps=[o for o in S.ops if o.is_dma and o.sem == "const"])

        S.finalize(sems)
        with nc.Block() as block:
            if S.streams["pe"]:
                @block.tensor
                def _(e):
                    S.replay("pe", e, sems)

            if S.streams["act"]:
                @block.scalar
                def _(e):
                    S.replay("act", e, sems)

            if S.streams["dve"]:
                @block.vector
                def _(e):
                    S.replay("dve", e, sems)

            if S.streams["pool"]:
                @block.gpsimd
                def _(e):
                    S.replay("pool", e, sems)

            if S.streams["sp"]:
                @block.sync
                def _(e):
                    S.replay("sp", e, sems)
    return nc


def _prep_inputs(inp):
    x = np.asarray(inp["x"], np.float32)
    p = np.asarray(inp["p"], np.float32)[0]
    wpack = _pack_weights(np.asarray(inp["w_in"][0], np.float32), np.asarray(inp["w_out"][0], np.float32),
                          np.asarray(inp["ffn_w_up"][0], np.float32), np.asarray(inp["ffn_w_down"][0], np.float32),
                          np.asarray(inp["ple_w_proj"][0], np.float32), np.asarray(inp["ple_w_gate"][0], np.float32))
    cpack = _pack_consts(inp)
    maps = []
    for i in range(8):
        xs = x[2 * i:2 * i + 2].reshape(TOK, D)
        ps = p[2 * i:2 * i + 2].reshape(TOK, 256)
        maps.append({"xT": np.ascontiguousarray(xs.T), "pT": np.ascontiguousarray(ps.T),
                     "wpack": wpack, "cpack": cpack})
    return maps


def kernel(**inputs):
    maps = _prep_inputs(inputs)
    nc = build_nc()
    res = run_bass_kernel_spmd(nc, maps, core_ids=list(range(8)))
    out = np.empty((16, 2048, D), np.float32)
    for i in range(8):
        o = np.asarray(res.results[i]["outT"], np.float32)
        out[2 * i:2 * i + 2] = o.T.reshape(2, 2048, D)
    return out
```
